# Optimizing a Trainium2 kernel written in Bass

```python
import jax, jax.numpy as jnp
from jax import lax
import numpy as np

D_MODEL = 1024
BATCH = 2
SEQ = 8192
DEPTH = 1

CHUNK = 64
N_LEFT_CHUNKS = 8
BAND = (N_LEFT_CHUNKS + 1) * CHUNK
ATT_HEADS = 8
ATT_HEAD_DIM = 64
D_ATT = ATT_HEADS * ATT_HEAD_DIM
D_CONV = D_MODEL // 2
CONV_WIDTH = 3
MAX_REL = 128
N_REL = 2 * MAX_REL + 1
EPS = 1e-6
NEG_BIG = -1e30
IN_SIZES = (D_ATT, D_ATT, D_ATT, D_ATT, D_CONV, D_CONV, D_CONV, D_CONV, D_MODEL, D_MODEL)
IN_COLS = sum(IN_SIZES)
IN_SPLITS = tuple(int(s) for s in np.cumsum(IN_SIZES)[:-1])

kernel_name = "hybrid_chunk_attn_shortconv_gated_block"


def rmsnorm(x, g):
    xf = x.astype(jnp.float32)
    r = lax.rsqrt(jnp.mean(xf * xf, axis=-1, keepdims=True) + EPS)
    return (xf * r).astype(x.dtype) * g


def chunked_rel_attention(q, k, v, rel_bias):
    b, s, h, dh = q.shape
    nc = s // CHUNK
    qc = q.reshape(b, nc, CHUNK, h, dh)
    kc = k.reshape(b, nc, CHUNK, h, dh)
    vc = v.reshape(b, nc, CHUNK, h, dh)
    pad = ((0, 0), (N_LEFT_CHUNKS, 0), (0, 0), (0, 0), (0, 0))
    kp = jnp.pad(kc, pad)
    vp = jnp.pad(vc, pad)
    kb = jnp.concatenate([kp[:, j:j + nc] for j in range(N_LEFT_CHUNKS + 1)], axis=2)
    vb = jnp.concatenate([vp[:, j:j + nc] for j in range(N_LEFT_CHUNKS + 1)], axis=2)
    scale = ATT_HEAD_DIM ** -0.5
    scores = jnp.einsum('bcqhd,bckhd->bhcqk', qc, kb).astype(jnp.float32) * scale
    rel = np.arange(CHUNK)[:, None] + N_LEFT_CHUNKS * CHUNK - np.arange(BAND)[None, :]
    idx = np.clip(rel, -MAX_REL, MAX_REL) + MAX_REL
    bias = rel_bias[:, idx].astype(jnp.float32)
    kpos = np.arange(nc)[:, None] * CHUNK - N_LEFT_CHUNKS * CHUNK + np.arange(BAND)[None, :]
    valid = jnp.asarray(kpos >= 0)
    scores = jnp.where(valid[None, None, :, None, :], scores + bias[None, :, None], NEG_BIG)
    p = jax.nn.softmax(scores, axis=-1).astype(v.dtype)
    out = jnp.einsum('bhcqk,bckhd->bcqhd', p, vb)
    return out.reshape(b, s, h * dh)


def causal_depthwise_conv(u, w, bias):
    s = u.shape[1]
    up = jnp.pad(u, ((0, 0), (CONV_WIDTH - 1, 0), (0, 0)))
    return sum(w[t] * up[:, t:t + s] for t in range(CONV_WIDTH)) + bias


def setup_inputs(seed: int = 0) -> dict:
    key = jax.random.key(seed)
    ks = jax.random.split(key, 12)
    f32 = jnp.float32
    x = jax.random.normal(ks[0], (BATCH, SEQ, D_MODEL), f32)
    norm_g = 1.0 + 0.05 * jax.random.normal(ks[1], (DEPTH, D_MODEL), f32)
    w_in = jax.random.normal(ks[2], (DEPTH, D_MODEL, IN_COLS), f32) * D_MODEL ** -0.5
    rel_bias = 0.2 * jax.random.normal(ks[3], (DEPTH, ATT_HEADS, N_REL), f32)
    w_att_out = jax.random.normal(ks[4], (DEPTH, D_ATT, D_MODEL), f32) * D_ATT ** -0.5
    conv_w = jax.random.normal(ks[5], (DEPTH, CONV_WIDTH, D_CONV), f32) * CONV_WIDTH ** -0.5
    conv_b = 0.02 * jax.random.normal(ks[6], (DEPTH, D_CONV), f32)
    w_conv_out = jax.random.normal(ks[7], (DEPTH, D_CONV, D_MODEL), f32) * D_CONV ** -0.5
    w_out = jax.random.normal(ks[8], (DEPTH, D_MODEL, D_MODEL), f32) * D_MODEL ** -0.5
    final_norm_g = 1.0 + 0.05 * jax.random.normal(ks[9], (D_MODEL,), f32)
    return {"x": x, "norm_g": norm_g, "w_in": w_in, "rel_bias": rel_bias,
            "w_att_out": w_att_out, "conv_w": conv_w, "conv_b": conv_b,
            "w_conv_out": w_conv_out, "w_out": w_out, "final_norm_g": final_norm_g}


def reference(x, norm_g, w_in, rel_bias, w_att_out, conv_w, conv_b, w_conv_out, w_out, final_norm_g):
    b, s, _ = x.shape
    for l in range(DEPTH):
        h = rmsnorm(x, norm_g[l])
        proj = jnp.einsum('bsd,de->bse', h, w_in[l])
        q, k, v, z_att, gb, gc, u, z_conv, g_att, g_conv = jnp.split(proj, IN_SPLITS, axis=-1)
        att = chunked_rel_attention(q.reshape(b, s, ATT_HEADS, ATT_HEAD_DIM),
                                    k.reshape(b, s, ATT_HEADS, ATT_HEAD_DIM),
                                    v.reshape(b, s, ATT_HEADS, ATT_HEAD_DIM),
                                    rel_bias[l])
        y_att = jnp.einsum('bsc,cd->bsd', att * jax.nn.silu(z_att), w_att_out[l])
        vconv = causal_depthwise_conv(gc * u, conv_w[l], conv_b[l])
        y_conv = jnp.einsum('bsc,cd->bsd', gb * vconv * jax.nn.silu(z_conv), w_conv_out[l])
        m = jax.nn.sigmoid(g_att) * y_att + jax.nn.sigmoid(g_conv) * y_conv
        x = x + jnp.einsum('bsd,de->bse', m, w_out[l])
    return rmsnorm(x, final_norm_g)
```

```python
import numpy as np
from contextlib import ExitStack

import concourse.bass as bass
import concourse.mybir as mybir
from concourse.bass_utils import run_bass_kernel_spmd

F32 = mybir.dt.float32
BF16 = mybir.dt.bfloat16
AF = mybir.ActivationFunctionType
ALU = mybir.AluOpType

D = 1024
SEQ = 8192
NB = 2
TOK = 2048
HALO = 512
TB = 256
NBLK = TOK // TB
NH = 8
EPS = 1e-6
NEG = -30000.0
N_CORES = 8

C_Q, C_K, C_V, C_ZA, C_B, C_C, C_U, C_ZC, C_GA, C_GC = 0, 512, 1024, 1536, 2048, 2560, 3072, 3584, 4096, 5120


class Buf:
    __slots__ = ("name", "w", "rs", "excl")

    def __init__(self, name, excl=False):
        self.name = name
        self.w = None
        self.rs = []
        self.excl = excl


class Op:
    __slots__ = ("eng", "fn", "deps", "dma", "sig", "tick", "sem", "idx", "tag")

    def __init__(self, eng, fn, dma):
        self.eng = eng
        self.fn = fn
        self.dma = dma
        self.deps = []
        self.sig = False
        self.tick = 0
        self.sem = None
        self.idx = 0


ENGS = ("pe", "act", "dve", "pool", "sp")
ANNOTATE = False

FILL_MODE = "all"
P4_DEPTH = 2
N_CONV_IN_P4 = 0
TAB_ENG = "dve"


class Sched:
    def __init__(self):
        self.q = {e: [] for e in ENGS}
        self.n = 0
        self.out_keys = set()
        self.tag = ""

    def add(self, eng, fn, reads=(), writes=(), dma=None, is_out=False):
        op = Op(eng, fn, dma)
        op.idx = self.n
        op.tag = self.tag
        self.n += 1
        deps = []
        for b in reads:
            if b.w is not None:
                deps.append((b.w, "raw"))
            if b.excl:
                for r in b.rs:
                    if r.eng != eng:
                        deps.append((r, "xrd"))
        for b in writes:
            if b.w is not None:
                deps.append((b.w, "waw"))
            for r in b.rs:
                deps.append((r, "war"))
        for b in writes:
            b.w = op
            b.rs = []
        wset = set(id(b) for b in writes)
        for b in reads:
            if id(b) not in wset:
                b.rs.append(op)
        seen = set()
        for d, kind in deps:
            if d is op:
                continue
            need = True
            if d.dma is None and op.dma is None and d.eng == op.eng:
                if op.eng == "pe":
                    need = False
                elif kind != "raw":
                    need = False
            if not need:
                continue
            if id(d) in seen:
                continue
            seen.add(id(d))
            op.deps.append(d)
            d.sig = True
        if dma is not None and is_out:
            self.out_keys.add(dma)
        self.q[eng].append(op)
        return op

    def emit(self, nc, es):
        esem = {e: es.enter_context(nc.semaphore("sem_" + e)) for e in ENGS}
        dkeys = []
        for e in ENGS:
            for op in self.q[e]:
                if op.dma is not None and op.dma not in dkeys:
                    dkeys.append(op.dma)
        dsem = {k: es.enter_context(nc.semaphore("dsem_" + k)) for k in dkeys}
        dcnt = {k: 0 for k in dkeys}
        allops = []
        for e in ENGS:
            allops.extend(self.q[e])
        allops.sort(key=lambda o: o.idx)
        ecnt = {e: 0 for e in ENGS}
        for op in allops:
            if op.dma is not None:
                dcnt[op.dma] += 16
                op.tick = dcnt[op.dma]
                op.sem = dsem[op.dma]
            elif op.sig:
                ecnt[op.eng] += 1
                op.tick = ecnt[op.eng]
                op.sem = esem[op.eng]
        block = es.enter_context(nc.Block())
        out_final = [(dsem[k], dcnt[k]) for k in sorted(self.out_keys)]

        def run(eng_obj, ops, final=False):
            waited = {}
            for op in ops:
                for d in op.deps:
                    key = id(d.sem)
                    if waited.get(key, 0) < d.tick:
                        eng_obj.wait_ge(d.sem, d.tick)
                        waited[key] = d.tick
                ins = op.fn(eng_obj)
                if ANNOTATE and op.tag:
                    ins.annotate(op.tag)
                if op.dma is not None:
                    ins.then_inc(op.sem, 16)
                elif op.sig:
                    ins.then_inc(op.sem, 1)
            if final:
                for s, v in out_final:
                    eng_obj.wait_ge(s, v)

        q = self.q

        @block.tensor
        def _(eng):
            run(eng, q["pe"])

        @block.scalar
        def _(eng):
            run(eng, q["act"])

        @block.vector
        def _(eng):
            run(eng, q["dve"])

        @block.gpsimd
        def _(eng):
            run(eng, q["pool"])

        @block.sync
        def _(eng):
            run(eng, q["sp"], final=True)


def build_program(nblk=NBLK):
    nc = bass.Bass("TRN2", target_bir_lowering=False)

    def din(name, shape):
        return nc.dram_tensor(name, shape, F32, kind="ExternalInput").ap()

    x_d = din("x", [TOK, D])
    xh_d = din("xh", [HALO, D])
    w_in_d = din("w_in", [D, 6144])
    w_att_d = din("w_att", [512, D])
    w_conv_d = din("w_conv", [512, D])
    w_out_d = din("w_out", [D, D])
    fg_d = din("fg", [128, D])
    gcol_d = din("gcol", [128, 8])
    cfar_d = din("cfar", [128, 8])
    hmask_d = din("hmask", [128, 2])
    cwt_d = din("cwt", [128, 12])
    cb_d = din("cb", [128, 4])
    ident_d = din("ident", [128, 128])
    btab_d = din("btab", [128, 8 * 384])
    y_d = nc.dram_tensor("y", [TOK, D], F32, kind="ExternalOutput").ap()

    S = Sched()
    with ExitStack() as es:
        def sb(name, shape, dt):
            return es.enter_context(nc.sbuf_tensor("sb_" + name, shape, dt))

        Win = sb("Win", [128, 8, 6144], BF16)
        Watt = sb("Watt", [128, 4, 1024], BF16)
        Wconv = sb("Wconv", [128, 4, 1024], BF16)
        Wout = sb("Wout", [128, 8, 1024], BF16)
        fg = sb("fg", [128, D], F32)
        gcol = sb("gcol", [128, 8], F32)
        cfar = sb("cfar", [128, 8], F32)
        negc = sb("negc", [128, 8], F32)
        hmask = sb("hmask", [128, 2], F32)
        cwt = sb("cwt", [128, 12], F32)
        cb = sb("cb", [128, 4], F32)
        ident = sb("ident", [128, 128], BF16)
        tab = sb("tab", [128, 8, 384], BF16)
        mhalf = sb("mhalf", [128, 1], F32)
        xs = [sb("xs%d" % i, [128, D], F32) for i in range(2)]
        xn = [sb("xn%d" % i, [128, D], BF16) for i in range(2)]
        hT = [sb("hT%d" % i, [128, 8, TB], BF16) for i in range(2)]
        qTz = sb("qTz", [128, 4, 2, TB], BF16)
        kT = sb("kT", [128, 4, 1024], BF16)
        Vaug = sb("Vaug", [128, 8, 8, 65], BF16)
        sz2 = sb("sz2", [128, 4, TB], BF16)
        u_sb = sb("u_sb", [128, TB], F32)
        cu = sb("cu", [128, 4, TB + 2], F32)
        accb = sb("accb", [128, TB], F32)
        tzc = sb("tzc", [128, TB], F32)
        convg = [sb("convg%d" % i, [128, 4, TB], BF16) for i in range(2)]
        PT = [sb("PT%d" % i, [128, 640], BF16) for i in range(3)]
        att_tok = [sb("att_tok%d" % i, [128, 512], BF16) for i in range(2)]
        attgT = sb("attgT", [128, 4, TB], BF16)
        rc = sb("rc", [128, 8], F32)
        tg = sb("tg", [128, 512], F32)
        tz = tg
        mm = sb("mm", [128, 512], F32)
        mT = sb("mT", [128, 8, TB], BF16)
        stat = sb("stat", [128, 8], F32)
        ps = es.enter_context(nc.psum_tensor("ps", [128, 4096], F32))

        B = {}

        def buf(name):
            if name not in B:
                B[name] = Buf(name)
            return B[name]

        bWcg = [buf("Wcg%d" % i) for i in range(12)]
        bWatt, bWconv, bWout = buf("Watt"), buf("Wconv"), buf("Wout")
        bxs = [buf("xs0"), buf("xs1")]
        bxn = [buf("xn0"), buf("xn1")]
        bhT = [buf("hT0"), buf("hT1")]
        bqTz = buf("qTz")
        bkT = [buf("kT%d" % i) for i in range(4)]
        bV = [buf("V%d" % i) for i in range(4)]
        bbank = [buf("bank%d" % i) for i in range(8)]
        for b_ in bbank:
            b_.excl = True
        bPT = [buf("PT%d" % i) for i in range(3)]
        batt = [buf("att0"), buf("att1")]
        bconvg = [buf("convg0"), buf("convg1")]
        btz = buf("tg")

        def bank(i):
            return ps[:, i * 512:(i + 1) * 512]

        bank_pool = [[0, 1, 2, 3]]
        gb_ctr = [0]

        def next_bank():
            pool = bank_pool[0]
            i = pool[gb_ctr[0] % len(pool)]
            gb_ctr[0] += 1
            return i

        w_in_v = w_in_d.rearrange("(k p) c -> p k c", p=128)

        def load_cg(cg):
            S.add("pool", lambda e: e.dma_start(out=Win[:, :, cg * 512:(cg + 1) * 512],
                                                in_=w_in_v[:, :, cg * 512:(cg + 1) * 512]),
                  writes=[bWcg[cg]], dma="w%d" % cg)

        def load_w(W, src, b, key):
            S.add("pool", lambda e: e.dma_start(out=W[:], in_=src.rearrange("(k p) c -> p k c", p=128)),
                  writes=[b], dma=key)

        def small(dst, src, name, bufname=None):
            S.add("sp", lambda e: e.dma_start(out=dst, in_=src), writes=[buf(bufname or name)], dma="c_" + name)

        def setup_tables():
            S.add("dve", lambda e: e.tensor_scalar(out=negc[:], in0=cfar[:], scalar1=-1.0, scalar2=None, op0=ALU.mult),
                  reads=[buf("cfar")], writes=[buf("negc")])
            mT32 = mT[:].rearrange("p a b -> p (a b)").bitcast(F32)
            stage = [(mT32[:, 0:384], buf("mT"), "tb0"), (mT32[:, 384:768], buf("mT"), "tb1"),
                     (mm[:, 0:384], buf("mm"), "tb2"),
                     (convg[0][:].rearrange("p a b -> p (a b)").bitcast(F32)[:, 0:384], bconvg[0], "tb3"),
                     (convg[1][:].rearrange("p a b -> p (a b)").bitcast(F32)[:, 0:384], bconvg[1], "tb4"),
                     (attgT[:].rearrange("p a b -> p (a b)").bitcast(F32)[:, 0:384], buf("attgT"), "tb5")]
            order = [2, 3, 4, 5, 0, 1, 2, 3]
            for h in range(8):
                view, b_, key = stage[order[h]]
                S.add("sp", (lambda e, h=h, view=view: e.dma_start(out=view, in_=btab_d[:, h * 384:(h + 1) * 384])),
                      writes=[b_], dma=key)
                S.add("dve", (lambda e, h=h, view=view: e.tensor_scalar(
                    out=tab[:, h, :], in0=view, scalar1=negc[:, h:h + 1], scalar2=8.0, op0=ALU.add, op1=ALU.mult)),
                    reads=[b_, buf("negc")], writes=[buf("tab")])

        def phase1a(src_d, row0):
            for t in range(2):
                slot = t
                r0 = row0 + t * 128
                S.add("sp", (lambda e, slot=slot, r0=r0: e.dma_start(out=xs[slot][:], in_=src_d[r0:r0 + 128, :])),
                      writes=[bxs[slot]], dma="xs%d" % slot)
            for t in range(2):
                slot = t
                S.add("act", (lambda e, slot=slot, t=t: e.activation(
                    out=xn[t][:], in_=xs[slot][:], func=AF.Square, scale=1.0 / 32.0, accum_out=stat[:, t:t + 1])),
                    reads=[bxs[slot]], writes=[bxn[t], buf("ss%d" % t)])
            for t in range(2):
                S.add("pool", (lambda e, t=t: e.tensor_scalar(out=stat[:, 2 + t:3 + t], in0=stat[:, t:t + 1],
                                                              scalar1=EPS, scalar2=None, op0=ALU.add)),
                      reads=[buf("ss%d" % t)], writes=[buf("rv%d" % t)])
                S.add("pool", (lambda e, t=t: e.tensor_tensor(out=stat[:, 4 + t:5 + t], in0=stat[:, 2 + t:3 + t],
                                                              in1=mhalf[:], op=ALU.pow)),
                      reads=[buf("rv%d" % t), buf("mhalf")], writes=[buf("r%d" % t)])
            for t in range(2):
                slot = t
                S.add("act", (lambda e, slot=slot, t=t: e.activation(
                    out=xn[t][:], in_=xs[slot][:], func=AF.Identity, scale=stat[:, 4 + t:5 + t])),
                    reads=[bxs[slot], buf("r%d" % t)], writes=[bxn[t]])

        def phase1b(hslot):
            bA, bB = next_bank(), next_bank()
            vA = bank(bA).bitcast(BF16).rearrange("p (k t) -> p k t", k=4)
            vB = bank(bB).bitcast(BF16).rearrange("p (k t) -> p k t", k=4)
            for t in range(2):
                def tr(e, t=t):
                    ins = None
                    for k in range(8):
                        v = vA if k < 4 else vB
                        ins = e.transpose(v[:, k % 4, t * 128:(t + 1) * 128], xn[t][:, k * 128:(k + 1) * 128], ident[:])
                    return ins
                S.add("pe", tr, reads=[bxn[t], buf("ident")], writes=[bbank[bA], bbank[bB]])
            for half, (bi, v) in enumerate(((bA, vA), (bB, vB))):
                g_b = gcol[:, half * 4:(half + 1) * 4].unsqueeze(2).to_broadcast([128, 4, TB])
                S.add("dve", (lambda e, half=half, v=v, g_b=g_b: e.tensor_tensor(
                    out=hT[hslot][:, half * 4:(half + 1) * 4, :], in0=v, in1=g_b, op=ALU.mult)),
                    reads=[bbank[bi], buf("gcol")], writes=[bhT[hslot]])

        def proj_pair(hslot, col0, col1):
            bi = next_bank()

            def f(e):
                ins = None
                for half, c0 in enumerate((col0, col1)):
                    for k in range(8):
                        ins = e.matmul(bank(bi)[:, half * TB:(half + 1) * TB], lhsT=Win[:, k, c0:c0 + 128],
                                       rhs=hT[hslot][:, k, :], start=(k == 0), stop=(k == 7))
                return ins
            cgs = sorted(set((col0 // 512, col1 // 512)))
            S.add("pe", f, reads=[bhT[hslot]] + [bWcg[c] for c in cgs], writes=[bbank[bi]])
            return bi

        def chunks_kv(hslot, kb0):
            ring = kb0 % 8
            rs = ring // 2
            out = []
            for j in range(2):
                def ck(j=j):
                    bi = proj_pair(hslot, C_K + (2 * j) * 128, C_K + (2 * j + 1) * 128)
                    S.add("dve", (lambda e: e.tensor_copy(
                        out=kT[:, 2 * j:2 * j + 2, ring * 128:ring * 128 + TB],
                        in_=bank(bi).rearrange("p (a t) -> p a t", a=2))),
                        reads=[bbank[bi]], writes=[bkT[rs]])
                out.append(ck)
            for t in range(2):
                def cv(t=t):
                    bi = next_bank()

                    def f(e):
                        ins = None
                        for k in range(8):
                            ins = e.matmul(bank(bi), lhsT=hT[hslot][:, k, t * 128:(t + 1) * 128],
                                           rhs=Win[:, k, C_V:C_V + 512], start=(k == 0), stop=(k == 7))
                        return ins
                    S.add("pe", f, reads=[bhT[hslot], bWcg[2]], writes=[bbank[bi]])
                    S.add("act", (lambda e: e.activation(
                        out=Vaug[:, ring + t, :, 0:64], in_=bank(bi).rearrange("p (h d) -> p h d", h=8), func=AF.Copy)),
                        reads=[bbank[bi]], writes=[bV[rs]])
                out.append(cv)
            return out

        def phase2_qz(hslot):
            for j in range(2):
                bi = proj_pair(hslot, C_Q + (2 * j) * 128, C_Q + (2 * j + 1) * 128)
                for hh in range(2):
                    p0 = hh * 64
                    S.add("act", (lambda e, bi=bi, j=j, hh=hh, p0=p0: e.activation(
                        out=qTz[p0:p0 + 64, 2 * j:2 * j + 2, hh, :],
                        in_=bank(bi)[p0:p0 + 64, :].rearrange("p (a t) -> p a t", a=2), func=AF.Copy)),
                        reads=[bbank[bi]], writes=[bqTz])
            for j in range(2):
                bi = proj_pair(hslot, C_ZA + (2 * j) * 128, C_ZA + (2 * j + 1) * 128)
                S.add("act", (lambda e, bi=bi: e.activation(out=tz[:], in_=bank(bi), func=AF.Tanh, scale=0.5)),
                      reads=[bbank[bi]], writes=[btz])
                S.add("dve", (lambda e, bi=bi, j=j: e.scalar_tensor_tensor(
                    out=sz2[:, 2 * j:2 * j + 2, :], in0=tz[:].rearrange("p (a t) -> p a t", a=2), scalar=1.0,
                    in1=bank(bi).rearrange("p (a t) -> p a t", a=2), op0=ALU.add, op1=ALU.mult)),
                    reads=[bbank[bi], btz], writes=[buf("sz2")])

        def conv_hist_init(hslot):
            bi = next_bank()

            def f(e):
                ins = None
                for g, cbase in enumerate((C_C, C_U)):
                    for j in range(4):
                        for k in range(8):
                            o = g * 8 + j * 2
                            ins = e.matmul(bank(bi)[:, o:o + 2], lhsT=Win[:, k, cbase + j * 128:cbase + (j + 1) * 128],
                                           rhs=hT[hslot][:, k, TB - 2:TB], start=(k == 0), stop=(k == 7))
                return ins
            S.add("pe", f, reads=[bhT[hslot], bWcg[5], bWcg[6]], writes=[bbank[bi]])
            S.add("act", lambda e: e.activation(out=u_sb[:, 0:8], in_=bank(bi)[:, 8:16], func=AF.Copy),
                  reads=[bbank[bi]], writes=[buf("u_sb")])
            S.add("dve", lambda e: e.tensor_tensor(
                out=cu[:, :, TB:TB + 2], in0=bank(bi)[:, 0:8].rearrange("p (j t) -> p j t", j=4),
                in1=u_sb[:, 0:8].rearrange("p (j t) -> p j t", j=4), op=ALU.mult),
                reads=[bbank[bi], buf("u_sb")], writes=[buf("cu%d" % j) for j in range(4)])

        def chunks_conv(hslot, cslot):
            out = []
            for j in range(4):
                st = {}

                def cx(j=j, st=st):
                    bX = proj_pair(hslot, C_C + j * 128, C_U + j * 128)
                    st["bX"] = bX
                    bcu = buf("cu%d" % j)
                    S.add("act", (lambda e: e.activation(out=u_sb[:], in_=bank(bX)[:, TB:2 * TB], func=AF.Copy)),
                          reads=[bbank[bX]], writes=[buf("u_sb")])
                    S.add("dve", (lambda e: e.tensor_copy(out=cu[:, j, 0:2], in_=cu[:, j, TB:TB + 2])),
                          reads=[bcu], writes=[bcu])
                    S.add("dve", (lambda e: e.tensor_tensor(out=cu[:, j, 2:TB + 2], in0=bank(bX)[:, 0:TB],
                                                            in1=u_sb[:], op=ALU.mult)),
                          reads=[bbank[bX], buf("u_sb"), bcu], writes=[bcu])
                    S.add("act", (lambda e: e.activation(out=accb[:], in_=cu[:, j, 2:TB + 2], func=AF.Identity,
                                                         scale=cwt[:, j * 3 + 2:j * 3 + 3], bias=cb[:, j:j + 1])),
                          reads=[bcu, buf("cwt"), buf("cb")], writes=[buf("accb")])
                    S.add("dve", (lambda e: e.scalar_tensor_tensor(out=accb[:], in0=cu[:, j, 1:TB + 1],
                                                                   scalar=cwt[:, j * 3 + 1:j * 3 + 2], in1=accb[:],
                                                                   op0=ALU.mult, op1=ALU.add)),
                          reads=[bcu, buf("cwt"), buf("accb")], writes=[buf("accb")])
                    S.add("dve", (lambda e: e.scalar_tensor_tensor(out=accb[:], in0=cu[:, j, 0:TB],
                                                                   scalar=cwt[:, j * 3:j * 3 + 1], in1=accb[:],
                                                                   op0=ALU.mult, op1=ALU.add)),
                          reads=[bcu, buf("cwt"), buf("accb")], writes=[buf("accb")])
                out.append(cx)

                def cy(j=j, st=st):
                    bY = proj_pair(hslot, C_B + j * 128, C_ZC + j * 128)
                    S.add("act", (lambda e: e.activation(out=tzc[:], in_=bank(bY)[:, TB:2 * TB], func=AF.Tanh, scale=0.5)),
                          reads=[bbank[bY]], writes=[buf("tzc")])
                    S.add("dve", (lambda e: e.tensor_tensor(out=accb[:], in0=bank(bY)[:, 0:TB], in1=accb[:], op=ALU.mult)),
                          reads=[bbank[bY], buf("accb")], writes=[buf("accb")])
                    S.add("dve", (lambda e: e.scalar_tensor_tensor(out=tzc[:], in0=tzc[:], scalar=1.0,
                                                                   in1=bank(bY)[:, TB:2 * TB], op0=ALU.add, op1=ALU.mult)),
                          reads=[bbank[bY], buf("tzc")], writes=[buf("tzc")])
                    S.add("dve", (lambda e: e.tensor_tensor(out=convg[cslot][:, j, :], in0=accb[:], in1=tzc[:], op=ALU.mult)),
                          reads=[buf("accb"), buf("tzc")], writes=[bconvg[cslot]])
                out.append(cy)
            return out

        unit_ctr = [0]

        def phase4(blk, filler):
            units = [(2 * blk + qi, h) for qi in range(2) for h in range(NH)]
            base = unit_ctr[0]
            unit_ctr[0] += len(units)
            nfill = len(filler)
            fdone = 0

            def slots_kb(i):
                return [i - 3, i - 2, i - 4, i - 1, i]

            def rec_front(n, i, h):
                pt = n % 3
                A = 2 + 2 * (n % 3)
                sc = ps[:, A * 512:A * 512 + 640]
                jt, hv = h // 2, h % 2
                qcol = (i % 2) * 128
                kbs = slots_kb(i)

                def qk(e):
                    def mm_qk(s, **kw):
                        ring = kbs[s] % 8
                        return e.matmul(sc[:, s * 128:(s + 1) * 128], lhsT=kT[:, jt, ring * 128:(ring + 1) * 128],
                                        rhs=qTz[:, jt, hv, qcol:qcol + 128], **kw)
                    mm_qk(0, start=True, stop=True)
                    mm_qk(1, start=True, stop=True)
                    e.matmul(sc[:, 256:512], lhsT=ident[:], rhs=tab[:, h, 0:256], start=False, stop=False,
                             skip_group_check=True)
                    mm_qk(2, start=False, stop=False, skip_group_check=True)
                    mm_qk(3, start=False, stop=True, skip_group_check=True)
                    e.matmul(sc[:, 512:640], lhsT=ident[:], rhs=tab[:, h, 256:384], start=True, stop=False)
                    return mm_qk(4, start=False, stop=True)
                S.add("pe", qk, reads=[bqTz, buf("tab"), buf("ident")] + [bkT[(kb % 8) // 2] for kb in kbs],
                      writes=[bbank[A], bbank[A + 1]])
                runs = []
                for s, kb in enumerate(kbs):
                    hal = kb < 0
                    if runs and runs[-1][2] == hal:
                        runs[-1][1] = s + 1
                    else:
                        runs.append([s, s + 1, hal])
                for ri, (s0, s1, hal) in enumerate(runs):
                    if hal:
                        S.add("act", (lambda e, s0=s0, s1=s1, pt=pt, sc=sc: e.activation(
                            out=PT[pt][:, s0 * 128:s1 * 128], in_=sc[:, s0 * 128:s1 * 128], func=AF.Exp,
                            bias=hmask[:, 0:1], scale=0.125)),
                            reads=[bbank[A], bbank[A + 1], buf("hmask")], writes=[bPT[pt]])
                    else:
                        S.add("act", (lambda e, s0=s0, s1=s1, pt=pt, sc=sc: e.activation(
                            out=PT[pt][:, s0 * 128:s1 * 128], in_=sc[:, s0 * 128:s1 * 128], func=AF.Exp,
                            scale=0.125)),
                            reads=[bbank[A], bbank[A + 1]], writes=[bPT[pt]])

            def rec_back(n, i, h):
                pt = n % 3
                Bk = n % 2
                acc = ps[:, Bk * 512:Bk * 512 + 65]
                kbs = slots_kb(i)
                asl = i % 2

                def pv(e):
                    ins = None
                    for s, kb in enumerate(kbs):
                        ring = kb % 8
                        ins = e.matmul(acc, lhsT=PT[pt][:, s * 128:(s + 1) * 128], rhs=Vaug[:, ring, h, :],
                                       start=(s == 0), stop=(s == 4))
                    return ins
                S.add("pe", pv, reads=[bPT[pt]] + [bV[(kb % 8) // 2] for kb in kbs], writes=[bbank[Bk]])
                S.add("dve", (lambda e, h=h, acc=acc: e.reciprocal(out=rc[:, h:h + 1], in_=acc[:, 64:65])),
                      reads=[bbank[Bk]], writes=[buf("rc%d" % h)])
                S.add("dve", (lambda e, h=h, acc=acc, asl=asl: e.tensor_scalar(
                    out=att_tok[asl][:, h * 64:(h + 1) * 64], in0=acc[:, 0:64], scalar1=rc[:, h:h + 1], scalar2=None,
                    op0=ALU.mult)),
                    reads=[bbank[Bk], buf("rc%d" % h)], writes=[batt[asl]])
                if h == NH - 1:
                    trv = ps[:, Bk * 512 + 256:Bk * 512 + 512].bitcast(BF16).rearrange("p (j t) -> p j t", j=4)

                    def trf(e):
                        ins = None
                        for j in range(4):
                            ins = e.transpose(trv[:, j, :], att_tok[asl][:, j * 128:(j + 1) * 128], ident[:])
                        return ins
                    S.add("pe", trf, reads=[batt[asl], buf("ident")], writes=[bbank[Bk]])
                    qcol = (i % 2) * 128
                    S.add("dve", (lambda e, trv=trv, qcol=qcol: e.tensor_tensor(
                        out=attgT[:, :, qcol:qcol + 128], in0=trv, in1=sz2[:, :, qcol:qcol + 128], op=ALU.mult)),
                        reads=[bbank[Bk], buf("sz2")], writes=[buf("attgT")])

            bank_pool[0] = [0, 1]
            for k in range(min(P4_DEPTH, len(units))):
                rec_front(base + k, *units[k])
            for idx, (i, h) in enumerate(units):
                if idx + P4_DEPTH < len(units):
                    rec_front(base + idx + P4_DEPTH, *units[idx + P4_DEPTH])
                want = ((idx + 1) * nfill + len(units) - 1) // len(units)
                while fdone < min(want, nfill):
                    filler[fdone]()
                    fdone += 1
                rec_back(base + idx, i, h)
            while fdone < nfill:
                filler[fdone]()
                fdone += 1
            bank_pool[0] = [0, 1, 2, 3]

        def phase5(hslot, cslot, filler=()):
            bank_pool[0] = [0, 1, 2, 3, 4, 5, 6, 7]
            filler = list(filler)
            for j in range(8):
                bG = proj_pair(hslot, C_GA + j * 128, C_GC + j * 128)
                bY = next_bank()

                def f(e, bY=bY, j=j):
                    ins = None
                    for half, (W, src) in enumerate(((Watt, attgT), (Wconv, convg[cslot]))):
                        for k in range(4):
                            ins = e.matmul(bank(bY)[:, half * TB:(half + 1) * TB], lhsT=W[:, k, j * 128:(j + 1) * 128],
                                           rhs=src[:, k, :], start=(k == 0), stop=(k == 3))
                    return ins
                S.add("pe", f, reads=[bWatt, bWconv, buf("attgT"), bconvg[cslot]], writes=[bbank[bY]])
                S.add("act", (lambda e, bG=bG: e.activation(out=tg[:], in_=bank(bG), func=AF.Tanh, scale=0.5)),
                      reads=[bbank[bG]], writes=[btz])
                S.add("dve", (lambda e, bY=bY: e.scalar_tensor_tensor(out=mm[:], in0=tg[:], scalar=1.0, in1=bank(bY),
                                                                      op0=ALU.add, op1=ALU.mult)),
                      reads=[bbank[bY], btz], writes=[buf("mm")])
                S.add("pool", (lambda e, j=j: e.tensor_tensor(out=mT[:, j, :], in0=mm[:, 0:TB], in1=mm[:, TB:2 * TB],
                                                              op=ALU.add)),
                      reads=[buf("mm")], writes=[buf("mT")])
                if j < len(filler):
                    filler[j]()
            for c in filler[8:]:
                c()
            bank_pool[0] = [0, 1, 2, 3]

        def phase6_load(blk):
            for t in range(2):
                slot = t
                r0 = blk * TB + t * 128
                S.add("sp", (lambda e, slot=slot, r0=r0: e.dma_start(out=xs[slot][:], in_=x_d[r0:r0 + 128, :])),
                      writes=[bxs[slot]], dma="xs%d" % slot)

        def phase6(blk):
            for t in range(2):
                slot = t
                r0 = blk * TB + t * 128
                for c in range(2):
                    bi = next_bank()

                    def f(e, bi=bi, t=t, c=c):
                        ins = None
                        for k in range(8):
                            ins = e.matmul(bank(bi), lhsT=mT[:, k, t * 128:(t + 1) * 128],
                                           rhs=Wout[:, k, c * 512:(c + 1) * 512], start=(k == 0), stop=(k == 7))
                        return ins
                    S.add("pe", f, reads=[buf("mT"), bWout], writes=[bbank[bi]])
                    S.add("dve", (lambda e, bi=bi, slot=slot, c=c: e.scalar_tensor_tensor(
                        out=xs[slot][:, c * 512:(c + 1) * 512], in0=bank(bi), scalar=0.25,
                        in1=xs[slot][:, c * 512:(c + 1) * 512], op0=ALU.mult, op1=ALU.add)),
                        reads=[bbank[bi], bxs[slot]], writes=[bxs[slot]])
            for t in range(2):
                slot = t
                r0 = blk * TB + t * 128
                junk, bjunk = ((mm, buf("mm")), (tg, btz))[t]
                S.add("act", (lambda e, slot=slot, t=t, junk=junk: e.activation(
                    out=junk[:].bitcast(BF16), in_=xs[slot][:], func=AF.Square, scale=1.0 / 32.0,
                    accum_out=stat[:, 6 + t:7 + t])),
                    reads=[bxs[slot]], writes=[bjunk, buf("fss%d" % t)])
                S.add("pool", (lambda e, t=t: e.tensor_scalar(out=stat[:, 6 + t:7 + t], in0=stat[:, 6 + t:7 + t],
                                                              scalar1=EPS, scalar2=None, op0=ALU.add)),
                      reads=[buf("fss%d" % t)], writes=[buf("fss%d" % t)])
                S.add("pool", (lambda e, t=t: e.tensor_tensor(out=stat[:, 6 + t:7 + t], in0=stat[:, 6 + t:7 + t],
                                                              in1=mhalf[:], op=ALU.pow)),
                      reads=[buf("fss%d" % t), buf("mhalf")], writes=[buf("fss%d" % t)])
                S.add("dve", (lambda e, slot=slot, t=t: e.scalar_tensor_tensor(
                    out=xs[slot][:], in0=xs[slot][:], scalar=stat[:, 6 + t:7 + t], in1=fg[:], op0=ALU.mult, op1=ALU.mult)),
                    reads=[bxs[slot], buf("fss%d" % t), buf("fg")], writes=[bxs[slot]])
                S.add("sp", (lambda e, slot=slot, r0=r0: e.dma_start(out=y_d[r0:r0 + 128, :], in_=xs[slot][:])),
                      reads=[bxs[slot]], dma="xs%d" % slot, is_out=True)

        def HS(b):
            return (b + 1) % 2

        S.tag = "setup"
        S.add("pool", lambda e: e.memset(mhalf[:], -0.5), writes=[buf("mhalf")])
        small(gcol[:], gcol_d, "gcol")
        small(tg[:, 0:128], ident_d, "identf", "tg")
        for cg in (1, 2, 0, 3):
            load_cg(cg)
        S.tag = "h0"
        phase1a(xh_d, 0)
        for cg in (5, 6, 4, 7):
            load_cg(cg)
        S.add("dve", lambda e: e.memset(Vaug[:, :, :, 64:65], 1.0), writes=bV)
        S.add("dve", lambda e: e.memset(qTz[64:128, :, 0, :], 0.0), writes=[bqTz])
        S.add("dve", lambda e: e.memset(qTz[0:64, :, 1, :], 0.0), writes=[bqTz])
        S.add("dve", lambda e: e.tensor_copy(out=ident[:], in_=tg[:, 0:128]), reads=[btz], writes=[buf("ident")])
        phase1b(0)
        S.tag = "b0.P1"
        phase1a(x_d, 0)
        load_w(Watt, w_att_d, bWatt, "watt")
        load_w(Wconv, w_conv_d, bWconv, "wconv")
        for cg in (8, 10, 9, 11):
            load_cg(cg)
        load_w(Wout, w_out_d, bWout, "wout")
        small(hmask[:], hmask_d, "hmask")
        small(cfar[:], cfar_d, "cfar")
        small(cwt[:], cwt_d, "cwt")
        small(cb[:], cb_d, "cb")
        S.tag = "h0"
        for c in chunks_kv(0, -4):
            c()
        S.tag = "b0.P1"
        phase1b(HS(0))
        S.tag = "setup"
        setup_tables()
        S.tag = "b0.P2"
        for c in chunks_kv(HS(0), 0):
            c()
        S.tag = "h1"
        phase1a(xh_d, TB)
        S.tag = "b0.P2"
        phase2_qz(HS(0))
        S.tag = "h1"
        phase1b(0)
        for c in chunks_kv(0, -2):
            c()
        conv_hist_init(0)
        if nblk > 1:
            S.tag = "b1.P1"
            phase1a(x_d, TB)
        S.tag = "setup"
        small(fg[:], fg_d, "fg")
        S.tag = "b0.P3"
        for c in chunks_conv(HS(0), 0):
            c()
        if nblk > 1:
            S.tag = "b1.P1"
            phase1b(HS(1))
        for blk in range(nblk):
            hs = HS(blk)
            nb = blk + 1
            S.tag = "b%d.P4" % blk
            fill4, fill5 = [], []
            if nb < nblk:
                ccv = chunks_conv(HS(nb), nb % 2)
                fill4 = chunks_kv(HS(nb), 2 * nb) + ccv[:N_CONV_IN_P4]
                fill5 = ccv[N_CONV_IN_P4:]
            phase4(blk, fill4)
            if nb < nblk:
                S.tag = "b%d.P2q" % nb
                phase2_qz(HS(nb))
            if blk + 2 < nblk:
                S.tag = "b%d.P1" % (blk + 2)
                phase1a(x_d, (blk + 2) * TB)
            S.tag = "b%d.P6" % blk
            phase6_load(blk)
            S.tag = "b%d.P5" % blk
            phase5(hs, blk % 2, fill5)
            if blk + 2 < nblk:
                S.tag = "b%d.P1" % (blk + 2)
                phase1b(hs)
            S.tag = "b%d.P6" % blk
            phase6(blk)

        S.emit(nc, es)
    return nc


def _bias_index_table():
    p = np.arange(128)[:, None]
    jq = np.arange(128)[None, :]
    far = np.full((128, 128), 256, dtype=np.int64)
    far[(p < 64) & (jq >= 64)] = -1
    relA = 128 + jq - p
    nearA = np.minimum(relA, 128) + 128
    relB = jq - p
    nearB = relB + 128
    nearB = np.where((p >= 64) & (jq < 64), -1, nearB)
    return np.concatenate([far, nearA, nearB], axis=1)


def kernel(x, norm_g, w_in, rel_bias, w_att_out, conv_w, conv_b, w_conv_out, w_out, final_norm_g):
    x = np.asarray(x, dtype=np.float32)
    f32 = np.float32
    w_in0 = np.ascontiguousarray(np.asarray(w_in, f32)[0])
    w_att0 = np.ascontiguousarray(np.asarray(w_att_out, f32)[0])
    w_conv0 = np.ascontiguousarray(np.asarray(w_conv_out, f32)[0])
    w_out0 = np.ascontiguousarray(np.asarray(w_out, f32)[0])
    g = np.asarray(norm_g, f32)[0]
    fgv = np.asarray(final_norm_g, f32)
    rb = np.asarray(rel_bias, f32)[0]
    cw = np.asarray(conv_w, f32)[0]
    cbv = np.asarray(conv_b, f32)[0]

    fg_bc = np.ascontiguousarray(np.broadcast_to(fgv[None, :], (128, D)))
    gcol = np.ascontiguousarray(g.reshape(8, 128).T)
    cfar = np.ascontiguousarray(np.broadcast_to(rb[:, 256][None, :], (128, 8)))
    cwt = np.ascontiguousarray(cw.reshape(3, 4, 128).transpose(2, 1, 0).reshape(128, 12))
    cbt = np.ascontiguousarray(cbv.reshape(4, 128).T)
    ident = np.eye(128, dtype=f32)
    idx = _bias_index_table()
    gathered = rb[:, np.maximum(idx, 0)]
    btab = np.where(idx[None, :, :] >= 0, gathered, f32(NEG)).astype(f32)
    btab = np.ascontiguousarray(btab.transpose(1, 0, 2).reshape(128, 8 * 384))

    in_maps = []
    for c in range(N_CORES):
        b, seg = c // 4, c % 4
        t0 = seg * TOK
        xc = np.ascontiguousarray(x[b, t0:t0 + TOK, :])
        if seg == 0:
            xh = np.zeros((HALO, D), f32)
            hm = np.tile(np.array([[NEG, 0.0]], f32), (128, 1))
        else:
            xh = np.ascontiguousarray(x[b, t0 - HALO:t0, :])
            hm = np.tile(np.array([[0.0, 1.0]], f32), (128, 1))
        in_maps.append({
            "x": xc, "xh": xh, "w_in": w_in0, "w_att": w_att0, "w_conv": w_conv0, "w_out": w_out0,
            "fg": fg_bc, "gcol": gcol, "cfar": cfar, "hmask": np.ascontiguousarray(hm), "cwt": cwt, "cb": cbt,
            "ident": ident, "btab": btab,
        })
    nc = build_program()
    res = run_bass_kernel_spmd(nc, in_maps, core_ids=list(range(N_CORES)))
    out = np.empty((NB, SEQ, D), dtype=np.float32)
    for c in range(N_CORES):
        b, seg = c // 4, c % 4
        out[b, seg * TOK:(seg + 1) * TOK, :] = res.results[c]["y"]
    return out
```

```python
import numpy as np
from contextlib import ExitStack

import concourse.bass as bass
import concourse.mybir as mybir
from concourse.bass_utils import run_bass_kernel_spmd

F32 = mybir.dt.float32
BF16 = mybir.dt.bfloat16
AF = mybir.ActivationFunctionType
ALU = mybir.AluOpType

D = 1024
SEQ = 8192
NB = 2
TOK = 2048
HALO = 512
TB = 256
NBLK = TOK // TB
NH = 8
EPS = 1e-6
NEG = -30000.0
N_CORES = 8

C_Q, C_K, C_V, C_ZA, C_B, C_C, C_U, C_ZC, C_GA, C_GC = 0, 512, 1024, 1536, 2048, 2560, 3072, 3584, 4096, 5120


class Buf:
    __slots__ = ("name", "w", "rs", "excl")

    def __init__(self, name, excl=False):
        self.name = name
        self.w = None
        self.rs = []
        self.excl = excl


class Op:
    __slots__ = ("eng", "fn", "deps", "dma", "sig", "tick", "sem", "idx", "tag")

    def __init__(self, eng, fn, dma):
        self.eng = eng
        self.fn = fn
        self.dma = dma
        self.deps = []
        self.sig = False
        self.tick = 0
        self.sem = None
        self.idx = 0


ENGS = ("pe", "act", "dve", "pool", "sp")
ANNOTATE = False

FILL_MODE = "all"
P4_DEPTH = 2
N_CONV_IN_P4 = 0
TAB_ENG = "dve"


class Sched:
    def __init__(self):
        self.q = {e: [] for e in ENGS}
        self.n = 0
        self.out_keys = set()
        self.tag = ""

    def add(self, eng, fn, reads=(), writes=(), dma=None, is_out=False):
        op = Op(eng, fn, dma)
        op.idx = self.n
        op.tag = self.tag
        self.n += 1
        deps = []
        for b in reads:
            if b.w is not None:
                deps.append((b.w, "raw"))
            if b.excl:
                for r in b.rs:
                    if r.eng != eng:
                        deps.append((r, "xrd"))
        for b in writes:
            if b.w is not None:
                deps.append((b.w, "waw"))
            for r in b.rs:
                deps.append((r, "war"))
        for b in writes:
            b.w = op
            b.rs = []
        wset = set(id(b) for b in writes)
        for b in reads:
            if id(b) not in wset:
                b.rs.append(op)
        seen = set()
        for d, kind in deps:
            if d is op:
                continue
            need = True
            if d.dma is None and op.dma is None and d.eng == op.eng:
                if op.eng == "pe":
                    need = False
                elif kind != "raw":
                    need = False
            if not need:
                continue
            if id(d) in seen:
                continue
            seen.add(id(d))
            op.deps.append(d)
            d.sig = True
        if dma is not None and is_out:
            self.out_keys.add(dma)
        self.q[eng].append(op)
        return op

    def emit(self, nc, es):
        esem = {e: es.enter_context(nc.semaphore("sem_" + e)) for e in ENGS}
        dkeys = []
        for e in ENGS:
            for op in self.q[e]:
                if op.dma is not None and op.dma not in dkeys:
                    dkeys.append(op.dma)
        dsem = {k: es.enter_context(nc.semaphore("dsem_" + k)) for k in dkeys}
        dcnt = {k: 0 for k in dkeys}
        allops = []
        for e in ENGS:
            allops.extend(self.q[e])
        allops.sort(key=lambda o: o.idx)
        ecnt = {e: 0 for e in ENGS}
        for op in allops:
            if op.dma is not None:
                dcnt[op.dma] += 16
                op.tick = dcnt[op.dma]
                op.sem = dsem[op.dma]
            elif op.sig:
                ecnt[op.eng] += 1
                op.tick = ecnt[op.eng]
                op.sem = esem[op.eng]
        block = es.enter_context(nc.Block())
        out_final = [(dsem[k], dcnt[k]) for k in sorted(self.out_keys)]

        def run(eng_obj, ops, final=False):
            waited = {}
            for op in ops:
                for d in op.deps:
                    key = id(d.sem)
                    if waited.get(key, 0) < d.tick:
                        eng_obj.wait_ge(d.sem, d.tick)
                        waited[key] = d.tick
                ins = op.fn(eng_obj)
                if ANNOTATE and op.tag:
                    ins.annotate(op.tag)
                if op.dma is not None:
                    ins.then_inc(op.sem, 16)
                elif op.sig:
                    ins.then_inc(op.sem, 1)
            if final:
                for s, v in out_final:
                    eng_obj.wait_ge(s, v)

        q = self.q

        @block.tensor
        def _(eng):
            run(eng, q["pe"])

        @block.scalar
        def _(eng):
            run(eng, q["act"])

        @block.vector
        def _(eng):
            run(eng, q["dve"])

        @block.gpsimd
        def _(eng):
            run(eng, q["pool"])

        @block.sync
        def _(eng):
            run(eng, q["sp"], final=True)


def build_program(nblk=NBLK):
    nc = bass.Bass("TRN2", target_bir_lowering=False)

    def din(name, shape):
        return nc.dram_tensor(name, shape, F32, kind="ExternalInput").ap()

    x_d = din("x", [TOK, D])
    xh_d = din("xh", [HALO, D])
    w_in_d = din("w_in", [D, 6144])
    w_att_d = din("w_att", [512, D])
    w_conv_d = din("w_conv", [512, D])
    w_out_d = din("w_out", [D, D])
    fg_d = din("fg", [128, D])
    gcol_d = din("gcol", [128, 8])
    cfar_d = din("cfar", [128, 8])
    hmask_d = din("hmask", [128, 2])
    cwt_d = din("cwt", [128, 12])
    cb_d = din("cb", [128, 4])
    ident_d = din("ident", [128, 128])
    btab_d = din("btab", [128, 8 * 384])
    y_d = nc.dram_tensor("y", [TOK, D], F32, kind="ExternalOutput").ap()

    S = Sched()
    with ExitStack() as es:
        def sb(name, shape, dt):
            return es.enter_context(nc.sbuf_tensor("sb_" + name, shape, dt))

        Win = sb("Win", [128, 8, 6144], BF16)
        Watt = sb("Watt", [128, 4, 1024], BF16)
        Wconv = sb("Wconv", [128, 4, 1024], BF16)
        Wout = sb("Wout", [128, 8, 1024], BF16)
        fg = sb("fg", [128, D], F32)
        gcol = sb("gcol", [128, 8], F32)
        cfar = sb("cfar", [128, 8], F32)
        negc = sb("negc", [128, 8], F32)
        hmask = sb("hmask", [128, 2], F32)
        cwt = sb("cwt", [128, 12], F32)
        cb = sb("cb", [128, 4], F32)
        ident = sb("ident", [128, 128], BF16)
        tab = sb("tab", [128, 8, 384], BF16)
        mhalf = sb("mhalf", [128, 1], F32)
        xs = [sb("xs%d" % i, [128, D], F32) for i in range(2)]
        xn = [sb("xn%d" % i, [128, D], BF16) for i in range(2)]
        hT = [sb("hT%d" % i, [128, 8, TB], BF16) for i in range(2)]
        qTz = sb("qTz", [128, 4, 2, TB], BF16)
        kT = sb("kT", [128, 4, 1024], BF16)
        Vaug = sb("Vaug", [128, 8, 8, 65], BF16)
        sz2 = sb("sz2", [128, 4, TB], BF16)
        u_sb = sb("u_sb", [128, TB], F32)
        cu = sb("cu", [128, 4, TB + 2], F32)
        accb = sb("accb", [128, TB], F32)
        tzc = sb("tzc", [128, TB], F32)
        convg = [sb("convg%d" % i, [128, 4, TB], BF16) for i in range(2)]
        PT = [sb("PT%d" % i, [128, 640], BF16) for i in range(3)]
        att_tok = [sb("att_tok%d" % i, [128, 512], BF16) for i in range(2)]
        attgT = sb("attgT", [128, 4, TB], BF16)
        rc = sb("rc", [128, 8], F32)
        tg = sb("tg", [128, 512], F32)
        tz = tg
        mm = sb("mm", [128, 512], F32)
        mT = sb("mT", [128, 8, TB], BF16)
        stat = sb("stat", [128, 8], F32)
        ps = es.enter_context(nc.psum_tensor("ps", [128, 4096], F32))

        B = {}

        def buf(name):
            if name not in B:
                B[name] = Buf(name)
            return B[name]

        bWcg = [buf("Wcg%d" % i) for i in range(12)]
        bWatt, bWconv, bWout = buf("Watt"), buf("Wconv"), buf("Wout")
        bxs = [buf("xs0"), buf("xs1")]
        bxn = [buf("xn0"), buf("xn1")]
        bhT = [buf("hT0"), buf("hT1")]
        bqTz = buf("qTz")
        bkT = [buf("kT%d" % i) for i in range(4)]
        bV = [buf("V%d" % i) for i in range(4)]
        bbank = [buf("bank%d" % i) for i in range(8)]
        for b_ in bbank:
            b_.excl = True
        bPT = [buf("PT%d" % i) for i in range(3)]
        batt = [buf("att0"), buf("att1")]
        bconvg = [buf("convg0"), buf("convg1")]
        btz = buf("tg")

        def bank(i):
            return ps[:, i * 512:(i + 1) * 512]

        bank_pool = [[0, 1, 2, 3]]
        gb_ctr = [0]

        def next_bank():
            pool = bank_pool[0]
            i = pool[gb_ctr[0] % len(pool)]
            gb_ctr[0] += 1
            return i

        w_in_v = w_in_d.rearrange("(k p) c -> p k c", p=128)

        def load_cg(cg):
            S.add("pool", lambda e: e.dma_start(out=Win[:, :, cg * 512:(cg + 1) * 512],
                                                in_=w_in_v[:, :, cg * 512:(cg + 1) * 512]),
                  writes=[bWcg[cg]], dma="w%d" % cg)

        def load_w(W, src, b, key):
            S.add("pool", lambda e: e.dma_start(out=W[:], in_=src.rearrange("(k p) c -> p k c", p=128)),
                  writes=[b], dma=key)

        def small(dst, src, name, bufname=None):
            S.add("sp", lambda e: e.dma_start(out=dst, in_=src), writes=[buf(bufname or name)], dma="c_" + name)

        def setup_tables():
            S.add("dve", lambda e: e.tensor_scalar(out=negc[:], in0=cfar[:], scalar1=-1.0, scalar2=None, op0=ALU.mult),
                  reads=[buf("cfar")], writes=[buf("negc")])
            mT32 = mT[:].rearrange("p a b -> p (a b)").bitcast(F32)
            stage = [(mT32[:, 0:384], buf("mT"), "tb0"), (mT32[:, 384:768], buf("mT"), "tb1"),
                     (mm[:, 0:384], buf("mm"), "tb2"),
                     (convg[0][:].rearrange("p a b -> p (a b)").bitcast(F32)[:, 0:384], bconvg[0], "tb3"),
                     (convg[1][:].rearrange("p a b -> p (a b)").bitcast(F32)[:, 0:384], bconvg[1], "tb4"),
                     (attgT[:].rearrange("p a b -> p (a b)").bitcast(F32)[:, 0:384], buf("attgT"), "tb5")]
            order = [2, 3, 4, 5, 0, 1, 2, 3]
            for h in range(8):
                view, b_, key = stage[order[h]]
                S.add("sp", (lambda e, h=h, view=view: e.dma_start(out=view, in_=btab_d[:, h * 384:(h + 1) * 384])),
                      writes=[b_], dma=key)
                S.add("dve", (lambda e, h=h, view=view: e.tensor_scalar(
                    out=tab[:, h, :], in0=view, scalar1=negc[:, h:h + 1], scalar2=8.0, op0=ALU.add, op1=ALU.mult)),
                    reads=[b_, buf("negc")], writes=[buf("tab")])

        def phase1a(src_d, row0):
            for t in range(2):
                slot = t
                r0 = row0 + t * 128
                S.add("sp", (lambda e, slot=slot, r0=r0: e.dma_start(out=xs[slot][:], in_=src_d[r0:r0 + 128, :])),
                      writes=[bxs[slot]], dma="xs%d" % slot)
            for t in range(2):
                slot = t
                S.add("act", (lambda e, slot=slot, t=t: e.activation(
                    out=xn[t][:], in_=xs[slot][:], func=AF.Square, scale=1.0 / 32.0, accum_out=stat[:, t:t + 1])),
                    reads=[bxs[slot]], writes=[bxn[t], buf("ss%d" % t)])
            for t in range(2):
                S.add("pool", (lambda e, t=t: e.tensor_scalar(out=stat[:, 2 + t:3 + t], in0=stat[:, t:t + 1],
                                                              scalar1=EPS, scalar2=None, op0=ALU.add)),
                      reads=[buf("ss%d" % t)], writes=[buf("rv%d" % t)])
                S.add("pool", (lambda e, t=t: e.tensor_tensor(out=stat[:, 4 + t:5 + t], in0=stat[:, 2 + t:3 + t],
                                                              in1=mhalf[:], op=ALU.pow)),
                      reads=[buf("rv%d" % t), buf("mhalf")], writes=[buf("r%d" % t)])
            for t in range(2):
                slot = t
                S.add("act", (lambda e, slot=slot, t=t: e.activation(
                    out=xn[t][:], in_=xs[slot][:], func=AF.Identity, scale=stat[:, 4 + t:5 + t])),
                    reads=[bxs[slot], buf("r%d" % t)], writes=[bxn[t]])

        def phase1b(hslot):
            bA, bB = next_bank(), next_bank()
            vA = bank(bA).bitcast(BF16).rearrange("p (k t) -> p k t", k=4)
            vB = bank(bB).bitcast(BF16).rearrange("p (k t) -> p k t", k=4)
            for t in range(2):
                def tr(e, t=t):
                    ins = None
                    for k in range(8):
                        v = vA if k < 4 else vB
                        ins = e.transpose(v[:, k % 4, t * 128:(t + 1) * 128], xn[t][:, k * 128:(k + 1) * 128], ident[:])
                    return ins
                S.add("pe", tr, reads=[bxn[t], buf("ident")], writes=[bbank[bA], bbank[bB]])
            for half, (bi, v) in enumerate(((bA, vA), (bB, vB))):
                g_b = gcol[:, half * 4:(half + 1) * 4].unsqueeze(2).to_broadcast([128, 4, TB])
                S.add("dve", (lambda e, half=half, v=v, g_b=g_b: e.tensor_tensor(
                    out=hT[hslot][:, half * 4:(half + 1) * 4, :], in0=v, in1=g_b, op=ALU.mult)),
                    reads=[bbank[bi], buf("gcol")], writes=[bhT[hslot]])

        def proj_pair(hslot, col0, col1):
            bi = next_bank()

            def f(e):
                ins = None
                for half, c0 in enumerate((col0, col1)):
                    for k in range(8):
                        ins = e.matmul(bank(bi)[:, half * TB:(half + 1) * TB], lhsT=Win[:, k, c0:c0 + 128],
                                       rhs=hT[hslot][:, k, :], start=(k == 0), stop=(k == 7))
                return ins
            cgs = sorted(set((col0 // 512, col1 // 512)))
            S.add("pe", f, reads=[bhT[hslot]] + [bWcg[c] for c in cgs], writes=[bbank[bi]])
            return bi

        def chunks_kv(hslot, kb0):
            ring = kb0 % 8
            rs = ring // 2
            out = []
            for j in range(2):
                def ck(j=j):
                    bi = proj_pair(hslot, C_K + (2 * j) * 128, C_K + (2 * j + 1) * 128)
                    S.add("dve", (lambda e: e.tensor_copy(
                        out=kT[:, 2 * j:2 * j + 2, ring * 128:ring * 128 + TB],
                        in_=bank(bi).rearrange("p (a t) -> p a t", a=2))),
                        reads=[bbank[bi]], writes=[bkT[rs]])
                out.append(ck)
            for t in range(2):
                def cv(t=t):
                    bi = next_bank()

                    def f(e):
                        ins = None
                        for k in range(8):
                            ins = e.matmul(bank(bi), lhsT=hT[hslot][:, k, t * 128:(t + 1) * 128],
                                           rhs=Win[:, k, C_V:C_V + 512], start=(k == 0), stop=(k == 7))
                        return ins
                    S.add("pe", f, reads=[bhT[hslot], bWcg[2]], writes=[bbank[bi]])
                    S.add("act", (lambda e: e.activation(
                        out=Vaug[:, ring + t, :, 0:64], in_=bank(bi).rearrange("p (h d) -> p h d", h=8), func=AF.Copy)),
                        reads=[bbank[bi]], writes=[bV[rs]])
                out.append(cv)
            return out

        def phase2_qz(hslot):
            for j in range(2):
                bi = proj_pair(hslot, C_Q + (2 * j) * 128, C_Q + (2 * j + 1) * 128)
                for hh in range(2):
                    p0 = hh * 64
                    S.add("act", (lambda e, bi=bi, j=j, hh=hh, p0=p0: e.activation(
                        out=qTz[p0:p0 + 64, 2 * j:2 * j + 2, hh, :],
                        in_=bank(bi)[p0:p0 + 64, :].rearrange("p (a t) -> p a t", a=2), func=AF.Copy)),
                        reads=[bbank[bi]], writes=[bqTz])
            for j in range(2):
                bi = proj_pair(hslot, C_ZA + (2 * j) * 128, C_ZA + (2 * j + 1) * 128)
                S.add("act", (lambda e, bi=bi: e.activation(out=tz[:], in_=bank(bi), func=AF.Tanh, scale=0.5)),
                      reads=[bbank[bi]], writes=[btz])
                S.add("dve", (lambda e, bi=bi, j=j: e.scalar_tensor_tensor(
                    out=sz2[:, 2 * j:2 * j + 2, :], in0=tz[:].rearrange("p (a t) -> p a t", a=2), scalar=1.0,
                    in1=bank(bi).rearrange("p (a t) -> p a t", a=2), op0=ALU.add, op1=ALU.mult)),
                    reads=[bbank[bi], btz], writes=[buf("sz2")])

        def conv_hist_init(hslot):
            bi = next_bank()

            def f(e):
                ins = None
                for g, cbase in enumerate((C_C, C_U)):
                    for j in range(4):
                        for k in range(8):
                            o = g * 8 + j * 2
                            ins = e.matmul(bank(bi)[:, o:o + 2], lhsT=Win[:, k, cbase + j * 128:cbase + (j + 1) * 128],
                                           rhs=hT[hslot][:, k, TB - 2:TB], start=(k == 0), stop=(k == 7))
                return ins
            S.add("pe", f, reads=[bhT[hslot], bWcg[5], bWcg[6]], writes=[bbank[bi]])
            S.add("act", lambda e: e.activation(out=u_sb[:, 0:8], in_=bank(bi)[:, 8:16], func=AF.Copy),
                  reads=[bbank[bi]], writes=[buf("u_sb")])
            S.add("dve", lambda e: e.tensor_tensor(
                out=cu[:, :, TB:TB + 2], in0=bank(bi)[:, 0:8].rearrange("p (j t) -> p j t", j=4),
                in1=u_sb[:, 0:8].rearrange("p (j t) -> p j t", j=4), op=ALU.mult),
                reads=[bbank[bi], buf("u_sb")], writes=[buf("cu%d" % j) for j in range(4)])

        def chunks_conv(hslot, cslot):
            out = []
            for j in range(4):
                st = {}

                def cx(j=j, st=st):
                    bX = proj_pair(hslot, C_C + j * 128, C_U + j * 128)
                    st["bX"] = bX
                    bcu = buf("cu%d" % j)
                    S.add("act", (lambda e: e.activation(out=u_sb[:], in_=bank(bX)[:, TB:2 * TB], func=AF.Copy)),
                          reads=[bbank[bX]], writes=[buf("u_sb")])
                    S.add("dve", (lambda e: e.tensor_copy(out=cu[:, j, 0:2], in_=cu[:, j, TB:TB + 2])),
                          reads=[bcu], writes=[bcu])
                    S.add("dve", (lambda e: e.tensor_tensor(out=cu[:, j, 2:TB + 2], in0=bank(bX)[:, 0:TB],
                                                            in1=u_sb[:], op=ALU.mult)),
                          reads=[bbank[bX], buf("u_sb"), bcu], writes=[bcu])
                    S.add("act", (lambda e: e.activation(out=accb[:], in_=cu[:, j, 2:TB + 2], func=AF.Identity,
                                                         scale=cwt[:, j * 3 + 2:j * 3 + 3], bias=cb[:, j:j + 1])),
                          reads=[bcu, buf("cwt"), buf("cb")], writes=[buf("accb")])
                    S.add("dve", (lambda e: e.scalar_tensor_tensor(out=accb[:], in0=cu[:, j, 1:TB + 1],
                                                                   scalar=cwt[:, j * 3 + 1:j * 3 + 2], in1=accb[:],
                                                                   op0=ALU.mult, op1=ALU.add)),
                          reads=[bcu, buf("cwt"), buf("accb")], writes=[buf("accb")])
                    S.add("dve", (lambda e: e.scalar_tensor_tensor(out=accb[:], in0=cu[:, j, 0:TB],
                                                                   scalar=cwt[:, j * 3:j * 3 + 1], in1=accb[:],
                                                                   op0=ALU.mult, op1=ALU.add)),
                          reads=[bcu, buf("cwt"), buf("accb")], writes=[buf("accb")])
                out.append(cx)

                def cy(j=j, st=st):
                    bY = proj_pair(hslot, C_B + j * 128, C_ZC + j * 128)
                    S.add("act", (lambda e: e.activation(out=tzc[:], in_=bank(bY)[:, TB:2 * TB], func=AF.Tanh, scale=0.5)),
                          reads=[bbank[bY]], writes=[buf("tzc")])
                    S.add("dve", (lambda e: e.tensor_tensor(out=accb[:], in0=bank(bY)[:, 0:TB], in1=accb[:], op=ALU.mult)),
                          reads=[bbank[bY], buf("accb")], writes=[buf("accb")])
                    S.add("dve", (lambda e: e.scalar_tensor_tensor(out=tzc[:], in0=tzc[:], scalar=1.0,
                                                                   in1=bank(bY)[:, TB:2 * TB], op0=ALU.add, op1=ALU.mult)),
                          reads=[bbank[bY], buf("tzc")], writes=[buf("tzc")])
                    S.add("dve", (lambda e: e.tensor_tensor(out=convg[cslot][:, j, :], in0=accb[:], in1=tzc[:], op=ALU.mult)),
                          reads=[buf("accb"), buf("tzc")], writes=[bconvg[cslot]])
                out.append(cy)
            return out

        unit_ctr = [0]

        def phase4(blk, filler):
            units = [(2 * blk + qi, h) for qi in range(2) for h in range(NH)]
            base = unit_ctr[0]
            unit_ctr[0] += len(units)
            nfill = len(filler)
            fdone = 0

            def slots_kb(i):
                return [i - 3, i - 2, i - 4, i - 1, i]

            def rec_front(n, i, h):
                pt = n % 3
                A = 2 + 2 * (n % 3)
                sc = ps[:, A * 512:A * 512 + 640]
                jt, hv = h // 2, h % 2
                qcol = (i % 2) * 128
                kbs = slots_kb(i)

                def qk(e):
                    def mm_qk(s, **kw):
                        ring = kbs[s] % 8
                        return e.matmul(sc[:, s * 128:(s + 1) * 128], lhsT=kT[:, jt, ring * 128:(ring + 1) * 128],
                                        rhs=qTz[:, jt, hv, qcol:qcol + 128], **kw)
                    mm_qk(0, start=True, stop=True)
                    mm_qk(1, start=True, stop=True)
                    e.matmul(sc[:, 256:512], lhsT=ident[:], rhs=tab[:, h, 0:256], start=False, stop=False,
                             skip_group_check=True)
                    mm_qk(2, start=False, stop=False, skip_group_check=True)
                    mm_qk(3, start=False, stop=True, skip_group_check=True)
                    e.matmul(sc[:, 512:640], lhsT=ident[:], rhs=tab[:, h, 256:384], start=True, stop=False)
                    return mm_qk(4, start=False, stop=True)
                S.add("pe", qk, reads=[bqTz, buf("tab"), buf("ident")] + [bkT[(kb % 8) // 2] for kb in kbs],
                      writes=[bbank[A], bbank[A + 1]])
                runs = []
                for s, kb in enumerate(kbs):
                    hal = kb < 0
                    if runs and runs[-1][2] == hal:
                        runs[-1][1] = s + 1
                    else:
                        runs.append([s, s + 1, hal])
                for ri, (s0, s1, hal) in enumerate(runs):
                    if hal:
                        S.add("act", (lambda e, s0=s0, s1=s1, pt=pt, sc=sc: e.activation(
                            out=PT[pt][:, s0 * 128:s1 * 128], in_=sc[:, s0 * 128:s1 * 128], func=AF.Exp,
                            bias=hmask[:, 0:1], scale=0.125)),
                            reads=[bbank[A], bbank[A + 1], buf("hmask")], writes=[bPT[pt]])
                    else:
                        S.add("act", (lambda e, s0=s0, s1=s1, pt=pt, sc=sc: e.activation(
                            out=PT[pt][:, s0 * 128:s1 * 128], in_=sc[:, s0 * 128:s1 * 128], func=AF.Exp,
                            scale=0.125)),
                            reads=[bbank[A], bbank[A + 1]], writes=[bPT[pt]])

            def rec_back(n, i, h):
                pt = n % 3
                Bk = n % 2
                acc = ps[:, Bk * 512:Bk * 512 + 65]
                kbs = slots_kb(i)
                asl = i % 2

                def pv(e):
                    ins = None
                    for s, kb in enumerate(kbs):
                        ring = kb % 8
                        ins = e.matmul(acc, lhsT=PT[pt][:, s * 128:(s + 1) * 128], rhs=Vaug[:, ring, h, :],
                                       start=(s == 0), stop=(s == 4))
                    return ins
                S.add("pe", pv, reads=[bPT[pt]] + [bV[(kb % 8) // 2] for kb in kbs], writes=[bbank[Bk]])
                S.add("dve", (lambda e, h=h, acc=acc: e.reciprocal(out=rc[:, h:h + 1], in_=acc[:, 64:65])),
                      reads=[bbank[Bk]], writes=[buf("rc%d" % h)])
                S.add("dve", (lambda e, h=h, acc=acc, asl=asl: e.tensor_scalar(
                    out=att_tok[asl][:, h * 64:(h + 1) * 64], in0=acc[:, 0:64], scalar1=rc[:, h:h + 1], scalar2=None,
                    op0=ALU.mult)),
                    reads=[bbank[Bk], buf("rc%d" % h)], writes=[batt[asl]])
                if h == NH - 1:
                    trv = ps[:, Bk * 512 + 256:Bk * 512 + 512].bitcast(BF16).rearrange("p (j t) -> p j t", j=4)

                    def trf(e):
                        ins = None
                        for j in range(4):
                            ins = e.transpose(trv[:, j, :], att_tok[asl][:, j * 128:(j + 1) * 128], ident[:])
                        return ins
                    S.add("pe", trf, reads=[batt[asl], buf("ident")], writes=[bbank[Bk]])
                    qcol = (i % 2) * 128
                    S.add("dve", (lambda e, trv=trv, qcol=qcol: e.tensor_tensor(
                        out=attgT[:, :, qcol:qcol + 128], in0=trv, in1=sz2[:, :, qcol:qcol + 128], op=ALU.mult)),
                        reads=[bbank[Bk], buf("sz2")], writes=[buf("attgT")])

            bank_pool[0] = [0, 1]
            for k in range(min(P4_DEPTH, len(units))):
                rec_front(base + k, *units[k])
            for idx, (i, h) in enumerate(units):
                if idx + P4_DEPTH < len(units):
                    rec_front(base + idx + P4_DEPTH, *units[idx + P4_DEPTH])
                want = ((idx + 1) * nfill + len(units) - 1) // len(units)
                while fdone < min(want, nfill):
                    filler[fdone]()
                    fdone += 1
                rec_back(base + idx, i, h)
            while fdone < nfill:
                filler[fdone]()
                fdone += 1
            bank_pool[0] = [0, 1, 2, 3]

        def phase5(hslot, cslot, filler=()):
            bank_pool[0] = [0, 1, 2, 3, 4, 5, 6, 7]
            filler = list(filler)
            for j in range(8):
                bG = proj_pair(hslot, C_GA + j * 128, C_GC + j * 128)
                bY = next_bank()

                def f(e, bY=bY, j=j):
                    ins = None
                    for half, (W, src) in enumerate(((Watt, attgT), (Wconv, convg[cslot]))):
                        for k in range(4):
                            ins = e.matmul(bank(bY)[:, half * TB:(half + 1) * TB], lhsT=W[:, k, j * 128:(j + 1) * 128],
                                           rhs=src[:, k, :], start=(k == 0), stop=(k == 3))
                    return ins
                S.add("pe", f, reads=[bWatt, bWconv, buf("attgT"), bconvg[cslot]], writes=[bbank[bY]])
                S.add("act", (lambda e, bG=bG: e.activation(out=tg[:], in_=bank(bG), func=AF.Tanh, scale=0.5)),
                      reads=[bbank[bG]], writes=[btz])
                S.add("dve", (lambda e, bY=bY: e.scalar_tensor_tensor(out=mm[:], in0=tg[:], scalar=1.0, in1=bank(bY),
                                                                      op0=ALU.add, op1=ALU.mult)),
                      reads=[bbank[bY], btz], writes=[buf("mm")])
                S.add("pool", (lambda e, j=j: e.tensor_tensor(out=mT[:, j, :], in0=mm[:, 0:TB], in1=mm[:, TB:2 * TB],
                                                              op=ALU.add)),
                      reads=[buf("mm")], writes=[buf("mT")])
                if j < len(filler):
                    filler[j]()
            for c in filler[8:]:
                c()
            bank_pool[0] = [0, 1, 2, 3]

        def phase6_load(blk):
            for t in range(2):
                slot = t
                r0 = blk * TB + t * 128
                S.add("sp", (lambda e, slot=slot, r0=r0: e.dma_start(out=xs[slot][:], in_=x_d[r0:r0 + 128, :])),
                      writes=[bxs[slot]], dma="xs%d" % slot)

        def phase6(blk):
            for t in range(2):
                slot = t
                r0 = blk * TB + t * 128
                for c in range(2):
                    bi = next_bank()

                    def f(e, bi=bi, t=t, c=c):
                        ins = None
                        for k in range(8):
                            ins = e.matmul(bank(bi), lhsT=mT[:, k, t * 128:(t + 1) * 128],
                                           rhs=Wout[:, k, c * 512:(c + 1) * 512], start=(k == 0), stop=(k == 7))
                        return ins
                    S.add("pe", f, reads=[buf("mT"), bWout], writes=[bbank[bi]])
                    S.add("dve", (lambda e, bi=bi, slot=slot, c=c: e.scalar_tensor_tensor(
                        out=xs[slot][:, c * 512:(c + 1) * 512], in0=bank(bi), scalar=0.25,
                        in1=xs[slot][:, c * 512:(c + 1) * 512], op0=ALU.mult, op1=ALU.add)),
                        reads=[bbank[bi], bxs[slot]], writes=[bxs[slot]])
            for t in range(2):
                slot = t
                r0 = blk * TB + t * 128
                junk, bjunk = ((mm, buf("mm")), (tg, btz))[t]
                S.add("act", (lambda e, slot=slot, t=t, junk=junk: e.activation(
                    out=junk[:].bitcast(BF16), in_=xs[slot][:], func=AF.Square, scale=1.0 / 32.0,
                    accum_out=stat[:, 6 + t:7 + t])),
                    reads=[bxs[slot]], writes=[bjunk, buf("fss%d" % t)])
                S.add("pool", (lambda e, t=t: e.tensor_scalar(out=stat[:, 6 + t:7 + t], in0=stat[:, 6 + t:7 + t],
                                                              scalar1=EPS, scalar2=None, op0=ALU.add)),
                      reads=[buf("fss%d" % t)], writes=[buf("fss%d" % t)])
                S.add("pool", (lambda e, t=t: e.tensor_tensor(out=stat[:, 6 + t:7 + t], in0=stat[:, 6 + t:7 + t],
                                                              in1=mhalf[:], op=ALU.pow)),
                      reads=[buf("fss%d" % t), buf("mhalf")], writes=[buf("fss%d" % t)])
                S.add("dve", (lambda e, slot=slot, t=t: e.scalar_tensor_tensor(
                    out=xs[slot][:], in0=xs[slot][:], scalar=stat[:, 6 + t:7 + t], in1=fg[:], op0=ALU.mult, op1=ALU.mult)),
                    reads=[bxs[slot], buf("fss%d" % t), buf("fg")], writes=[bxs[slot]])
                S.add("sp", (lambda e, slot=slot, r0=r0: e.dma_start(out=y_d[r0:r0 + 128, :], in_=xs[slot][:])),
                      reads=[bxs[slot]], dma="xs%d" % slot, is_out=True)

        def HS(b):
            return (b + 1) % 2

        S.tag = "setup"
        S.add("pool", lambda e: e.memset(mhalf[:], -0.5), writes=[buf("mhalf")])
        small(gcol[:], gcol_d, "gcol")
        small(tg[:, 0:128], ident_d, "identf", "tg")
        for cg in (1, 2, 0, 3):
            load_cg(cg)
        S.tag = "h0"
        phase1a(xh_d, 0)
        for cg in (5, 6, 4, 7):
            load_cg(cg)
        S.add("dve", lambda e: e.memset(Vaug[:, :, :, 64:65], 1.0), writes=bV)
        S.add("dve", lambda e: e.memset(qTz[64:128, :, 0, :], 0.0), writes=[bqTz])
        S.add("dve", lambda e: e.memset(qTz[0:64, :, 1, :], 0.0), writes=[bqTz])
        S.add("dve", lambda e: e.tensor_copy(out=ident[:], in_=tg[:, 0:128]), reads=[btz], writes=[buf("ident")])
        phase1b(0)
        S.tag = "b0.P1"
        phase1a(x_d, 0)
        load_w(Watt, w_att_d, bWatt, "watt")
        load_w(Wconv, w_conv_d, bWconv, "wconv")
        for cg in (8, 10, 9, 11):
            load_cg(cg)
        load_w(Wout, w_out_d, bWout, "wout")
        small(hmask[:], hmask_d, "hmask")
        small(cfar[:], cfar_d, "cfar")
        small(cwt[:], cwt_d, "cwt")
        small(cb[:], cb_d, "cb")
        S.tag = "h0"
        for c in chunks_kv(0, -4):
            c()
        S.tag = "b0.P1"
        phase1b(HS(0))
        S.tag = "setup"
        setup_tables()
        S.tag = "b0.P2"
        for c in chunks_kv(HS(0), 0):
            c()
        S.tag = "h1"
        phase1a(xh_d, TB)
        S.tag = "b0.P2"
        phase2_qz(HS(0))
        S.tag = "h1"
        phase1b(0)
        conv_hist_init(0)
        if nblk > 1:
            S.tag = "b1.P1"
            phase1a(x_d, TB)
        S.tag = "setup"
        small(fg[:], fg_d, "fg")
        S.tag = "b0.P3"
        for c in chunks_conv(HS(0), 0):
            c()
        S.tag = "h1"
        for c in chunks_kv(0, -2):
            c()
        if nblk > 1:
            S.tag = "b1.P1"
            phase1b(HS(1))
        for blk in range(nblk):
            hs = HS(blk)
            nb = blk + 1
            S.tag = "b%d.P4" % blk
            fill4, fill5 = [], []
            if nb < nblk:
                ccv = chunks_conv(HS(nb), nb % 2)
                fill4 = chunks_kv(HS(nb), 2 * nb) + ccv[:N_CONV_IN_P4]
                fill5 = ccv[N_CONV_IN_P4:]
            phase4(blk, fill4)
            if nb < nblk:
                S.tag = "b%d.P2q" % nb
                phase2_qz(HS(nb))
            if blk + 2 < nblk:
                S.tag = "b%d.P1" % (blk + 2)
                phase1a(x_d, (blk + 2) * TB)
            S.tag = "b%d.P6" % blk
            phase6_load(blk)
            S.tag = "b%d.P5" % blk
            phase5(hs, blk % 2, fill5)
            if blk + 2 < nblk:
                S.tag = "b%d.P1" % (blk + 2)
                phase1b(hs)
            S.tag = "b%d.P6" % blk
            phase6(blk)

        S.emit(nc, es)
    return nc


def _bias_index_table():
    p = np.arange(128)[:, None]
    jq = np.arange(128)[None, :]
    far = np.full((128, 128), 256, dtype=np.int64)
    far[(p < 64) & (jq >= 64)] = -1
    relA = 128 + jq - p
    nearA = np.minimum(relA, 128) + 128
    relB = jq - p
    nearB = relB + 128
    nearB = np.where((p >= 64) & (jq < 64), -1, nearB)
    return np.concatenate([far, nearA, nearB], axis=1)


def kernel(x, norm_g, w_in, rel_bias, w_att_out, conv_w, conv_b, w_conv_out, w_out, final_norm_g):
    x = np.asarray(x, dtype=np.float32)
    f32 = np.float32
    w_in0 = np.ascontiguousarray(np.asarray(w_in, f32)[0])
    w_att0 = np.ascontiguousarray(np.asarray(w_att_out, f32)[0])
    w_conv0 = np.ascontiguousarray(np.asarray(w_conv_out, f32)[0])
    w_out0 = np.ascontiguousarray(np.asarray(w_out, f32)[0])
    g = np.asarray(norm_g, f32)[0]
    fgv = np.asarray(final_norm_g, f32)
    rb = np.asarray(rel_bias, f32)[0]
    cw = np.asarray(conv_w, f32)[0]
    cbv = np.asarray(conv_b, f32)[0]

    fg_bc = np.ascontiguousarray(np.broadcast_to(fgv[None, :], (128, D)))
    gcol = np.ascontiguousarray(g.reshape(8, 128).T)
    cfar = np.ascontiguousarray(np.broadcast_to(rb[:, 256][None, :], (128, 8)))
    cwt = np.ascontiguousarray(cw.reshape(3, 4, 128).transpose(2, 1, 0).reshape(128, 12))
    cbt = np.ascontiguousarray(cbv.reshape(4, 128).T)
    ident = np.eye(128, dtype=f32)
    idx = _bias_index_table()
    gathered = rb[:, np.maximum(idx, 0)]
    btab = np.where(idx[None, :, :] >= 0, gathered, f32(NEG)).astype(f32)
    btab = np.ascontiguousarray(btab.transpose(1, 0, 2).reshape(128, 8 * 384))

    in_maps = []
    for c in range(N_CORES):
        b, seg = c // 4, c % 4
        t0 = seg * TOK
        xc = np.ascontiguousarray(x[b, t0:t0 + TOK, :])
        if seg == 0:
            xh = np.zeros((HALO, D), f32)
            hm = np.tile(np.array([[NEG, 0.0]], f32), (128, 1))
        else:
            xh = np.ascontiguousarray(x[b, t0 - HALO:t0, :])
            hm = np.tile(np.array([[0.0, 1.0]], f32), (128, 1))
        in_maps.append({
            "x": xc, "xh": xh, "w_in": w_in0, "w_att": w_att0, "w_conv": w_conv0, "w_out": w_out0,
            "fg": fg_bc, "gcol": gcol, "cfar": cfar, "hmask": np.ascontiguousarray(hm), "cwt": cwt, "cb": cbt,
            "ident": ident, "btab": btab,
        })
    nc = build_program()
    res = run_bass_kernel_spmd(nc, in_maps, core_ids=list(range(N_CORES)))
    out = np.empty((NB, SEQ, D), dtype=np.float32)
    for c in range(N_CORES):
        b, seg = c // 4, c % 4
        out[b, seg * TOK:(seg + 1) * TOK, :] = res.results[c]["y"]
    return out
```

```python
import numpy as np
from contextlib import ExitStack

import concourse.bass as bass
import concourse.mybir as mybir
from concourse.bass_utils import run_bass_kernel_spmd

F32 = mybir.dt.float32
BF16 = mybir.dt.bfloat16
AF = mybir.ActivationFunctionType
ALU = mybir.AluOpType

D = 1024
SEQ = 8192
NB = 2
TOK = 2048
HALO = 512
TB = 256
NBLK = TOK // TB
NH = 8
EPS = 1e-6
NEG = -30000.0
N_CORES = 8

C_Q, C_K, C_V, C_ZA, C_B, C_C, C_U, C_ZC, C_GA, C_GC = 0, 512, 1024, 1536, 2048, 2560, 3072, 3584, 4096, 5120


class Buf:
    __slots__ = ("name", "w", "rs", "excl")

    def __init__(self, name, excl=False):
        self.name = name
        self.w = None
        self.rs = []
        self.excl = excl


class Op:
    __slots__ = ("eng", "fn", "deps", "dma", "sig", "tick", "sem", "idx", "tag")

    def __init__(self, eng, fn, dma):
        self.eng = eng
        self.fn = fn
        self.dma = dma
        self.deps = []
        self.sig = False
        self.tick = 0
        self.sem = None
        self.idx = 0


ENGS = ("pe", "act", "dve", "pool", "sp")
ANNOTATE = False

FILL_MODE = "all"
P4_DEPTH = 2
N_CONV_IN_P4 = 0
TAB_ENG = "dve"


class Sched:
    def __init__(self):
        self.q = {e: [] for e in ENGS}
        self.n = 0
        self.out_keys = set()
        self.tag = ""

    def add(self, eng, fn, reads=(), writes=(), dma=None, is_out=False):
        op = Op(eng, fn, dma)
        op.idx = self.n
        op.tag = self.tag
        self.n += 1
        deps = []
        for b in reads:
            if b.w is not None:
                deps.append((b.w, "raw"))
            if b.excl:
                for r in b.rs:
                    if r.eng != eng:
                        deps.append((r, "xrd"))
        for b in writes:
            if b.w is not None:
                deps.append((b.w, "waw"))
            for r in b.rs:
                deps.append((r, "war"))
        for b in writes:
            b.w = op
            b.rs = []
        wset = set(id(b) for b in writes)
        for b in reads:
            if id(b) not in wset:
                b.rs.append(op)
        seen = set()
        for d, kind in deps:
            if d is op:
                continue
            need = True
            if d.dma is None and op.dma is None and d.eng == op.eng:
                if op.eng == "pe":
                    need = False
                elif kind != "raw":
                    need = False
            if not need:
                continue
            if id(d) in seen:
                continue
            seen.add(id(d))
            op.deps.append(d)
            d.sig = True
        if dma is not None and is_out:
            self.out_keys.add(dma)
        self.q[eng].append(op)
        return op

    def emit(self, nc, es):
        esem = {e: es.enter_context(nc.semaphore("sem_" + e)) for e in ENGS}
        dkeys = []
        for e in ENGS:
            for op in self.q[e]:
                if op.dma is not None and op.dma not in dkeys:
                    dkeys.append(op.dma)
        dsem = {k: es.enter_context(nc.semaphore("dsem_" + k)) for k in dkeys}
        dcnt = {k: 0 for k in dkeys}
        allops = []
        for e in ENGS:
            allops.extend(self.q[e])
        allops.sort(key=lambda o: o.idx)
        ecnt = {e: 0 for e in ENGS}
        for op in allops:
            if op.dma is not None:
                dcnt[op.dma] += 16
                op.tick = dcnt[op.dma]
                op.sem = dsem[op.dma]
            elif op.sig:
                ecnt[op.eng] += 1
                op.tick = ecnt[op.eng]
                op.sem = esem[op.eng]
        block = es.enter_context(nc.Block())
        out_final = [(dsem[k], dcnt[k]) for k in sorted(self.out_keys)]

        def run(eng_obj, ops, final=False):
            waited = {}
            for op in ops:
                for d in op.deps:
                    key = id(d.sem)
                    if waited.get(key, 0) < d.tick:
                        eng_obj.wait_ge(d.sem, d.tick)
                        waited[key] = d.tick
                ins = op.fn(eng_obj)
                if ANNOTATE and op.tag:
                    ins.annotate(op.tag)
                if op.dma is not None:
                    ins.then_inc(op.sem, 16)
                elif op.sig:
                    ins.then_inc(op.sem, 1)
            if final:
                for s, v in out_final:
                    eng_obj.wait_ge(s, v)

        q = self.q

        @block.tensor
        def _(eng):
            run(eng, q["pe"])

        @block.scalar
        def _(eng):
            run(eng, q["act"])

        @block.vector
        def _(eng):
            run(eng, q["dve"])

        @block.gpsimd
        def _(eng):
            run(eng, q["pool"])

        @block.sync
        def _(eng):
            run(eng, q["sp"], final=True)


def build_program(nblk=NBLK):
    nc = bass.Bass("TRN2", target_bir_lowering=False)

    def din(name, shape):
        return nc.dram_tensor(name, shape, F32, kind="ExternalInput").ap()

    x_d = din("x", [TOK, D])
    xh_d = din("xh", [HALO, D])
    w_in_d = din("w_in", [D, 6144])
    w_att_d = din("w_att", [512, D])
    w_conv_d = din("w_conv", [512, D])
    w_out_d = din("w_out", [D, D])
    fg_d = din("fg", [128, D])
    gcol_d = din("gcol", [128, 8])
    cfar_d = din("cfar", [128, 8])
    hmask_d = din("hmask", [128, 2])
    cwt_d = din("cwt", [128, 12])
    cb_d = din("cb", [128, 4])
    ident_d = din("ident", [128, 128])
    btab_d = din("btab", [128, 8 * 384])
    y_d = nc.dram_tensor("y", [TOK, D], F32, kind="ExternalOutput").ap()

    S = Sched()
    with ExitStack() as es:
        def sb(name, shape, dt):
            return es.enter_context(nc.sbuf_tensor("sb_" + name, shape, dt))

        Win = sb("Win", [128, 8, 6144], BF16)
        Watt = sb("Watt", [128, 4, 1024], BF16)
        Wconv = sb("Wconv", [128, 4, 1024], BF16)
        Wout = sb("Wout", [128, 8, 1024], BF16)
        fg = sb("fg", [128, D], F32)
        gcol = sb("gcol", [128, 8], F32)
        cfar = sb("cfar", [128, 8], F32)
        negc = sb("negc", [128, 8], F32)
        hmask = sb("hmask", [128, 2], F32)
        cwt = sb("cwt", [128, 12], F32)
        cb = sb("cb", [128, 4], F32)
        ident = sb("ident", [128, 128], BF16)
        tab = sb("tab", [128, 8, 384], BF16)
        mhalf = sb("mhalf", [128, 1], F32)
        xs = [sb("xs%d" % i, [128, D], F32) for i in range(2)]
        xn = [sb("xn%d" % i, [128, D], BF16) for i in range(2)]
        hT = [sb("hT%d" % i, [128, 8, TB], BF16) for i in range(2)]
        qTz = sb("qTz", [128, 4, 2, TB], BF16)
        kT = sb("kT", [128, 4, 1024], BF16)
        Vaug = sb("Vaug", [128, 8, 8, 65], BF16)
        sz2 = sb("sz2", [128, 4, TB], BF16)
        u_sb = sb("u_sb", [128, TB], F32)
        cu = sb("cu", [128, 4, TB + 2], F32)
        accb = sb("accb", [128, TB], F32)
        tzc = sb("tzc", [128, TB], F32)
        convg = [sb("convg%d" % i, [128, 4, TB], BF16) for i in range(2)]
        PT = [sb("PT%d" % i, [128, 640], BF16) for i in range(3)]
        att_tok = [sb("att_tok%d" % i, [128, 512], BF16) for i in range(2)]
        attgT = sb("attgT", [128, 4, TB], BF16)
        rc = sb("rc", [128, 8], F32)
        tg = sb("tg", [128, 512], F32)
        tz = tg
        mm = sb("mm", [128, 512], F32)
        mT = sb("mT", [128, 8, TB], BF16)
        stat = sb("stat", [128, 8], F32)
        ps = es.enter_context(nc.psum_tensor("ps", [128, 4096], F32))

        B = {}

        def buf(name):
            if name not in B:
                B[name] = Buf(name)
            return B[name]

        bWcg = [buf("Wcg%d" % i) for i in range(12)]
        bWatt, bWconv, bWout = buf("Watt"), buf("Wconv"), buf("Wout")
        bxs = [buf("xs0"), buf("xs1")]
        bxn = [buf("xn0"), buf("xn1")]
        bhT = [buf("hT0"), buf("hT1")]
        bqTz = buf("qTz")
        bkT = [buf("kT%d" % i) for i in range(4)]
        bV = [buf("V%d" % i) for i in range(4)]
        bbank = [buf("bank%d" % i) for i in range(8)]
        for b_ in bbank:
            b_.excl = True
        bPT = [buf("PT%d" % i) for i in range(3)]
        batt = [buf("att0"), buf("att1")]
        bconvg = [buf("convg0"), buf("convg1")]
        btz = buf("tg")

        def bank(i):
            return ps[:, i * 512:(i + 1) * 512]

        bank_pool = [[0, 1, 2, 3]]
        gb_ctr = [0]

        def next_bank():
            pool = bank_pool[0]
            i = pool[gb_ctr[0] % len(pool)]
            gb_ctr[0] += 1
            return i

        w_in_v = w_in_d.rearrange("(k p) c -> p k c", p=128)

        def load_cg(cg):
            S.add("pool", lambda e: e.dma_start(out=Win[:, :, cg * 512:(cg + 1) * 512],
                                                in_=w_in_v[:, :, cg * 512:(cg + 1) * 512]),
                  writes=[bWcg[cg]], dma="w%d" % cg)

        def load_w(W, src, b, key):
            S.add("pool", lambda e: e.dma_start(out=W[:], in_=src.rearrange("(k p) c -> p k c", p=128)),
                  writes=[b], dma=key)

        def small(dst, src, name, bufname=None):
            S.add("sp", lambda e: e.dma_start(out=dst, in_=src), writes=[buf(bufname or name)], dma="c_" + name)

        def setup_tables():
            S.add("dve", lambda e: e.tensor_scalar(out=negc[:], in0=cfar[:], scalar1=-1.0, scalar2=None, op0=ALU.mult),
                  reads=[buf("cfar")], writes=[buf("negc")])
            mT32 = mT[:].rearrange("p a b -> p (a b)").bitcast(F32)
            stage = [(mT32[:, 0:384], buf("mT"), "tb0"), (mT32[:, 384:768], buf("mT"), "tb1"),
                     (mm[:, 0:384], buf("mm"), "tb2"),
                     (convg[0][:].rearrange("p a b -> p (a b)").bitcast(F32)[:, 0:384], bconvg[0], "tb3"),
                     (convg[1][:].rearrange("p a b -> p (a b)").bitcast(F32)[:, 0:384], bconvg[1], "tb4"),
                     (attgT[:].rearrange("p a b -> p (a b)").bitcast(F32)[:, 0:384], buf("attgT"), "tb5")]
            order = [2, 3, 4, 5, 0, 1, 2, 3]
            for h in range(8):
                view, b_, key = stage[order[h]]
                S.add("sp", (lambda e, h=h, view=view: e.dma_start(out=view, in_=btab_d[:, h * 384:(h + 1) * 384])),
                      writes=[b_], dma=key)
                S.add("dve", (lambda e, h=h, view=view: e.tensor_scalar(
                    out=tab[:, h, :], in0=view, scalar1=negc[:, h:h + 1], scalar2=8.0, op0=ALU.add, op1=ALU.mult)),
                    reads=[b_, buf("negc")], writes=[buf("tab")])

        def phase1a(src_d, row0):
            for t in range(2):
                slot = t
                r0 = row0 + t * 128
                S.add("sp", (lambda e, slot=slot, r0=r0: e.dma_start(out=xs[slot][:], in_=src_d[r0:r0 + 128, :])),
                      writes=[bxs[slot]], dma="xs%d" % slot)
            for t in range(2):
                slot = t
                S.add("act", (lambda e, slot=slot, t=t: e.activation(
                    out=xn[t][:], in_=xs[slot][:], func=AF.Square, scale=1.0 / 32.0, accum_out=stat[:, t:t + 1])),
                    reads=[bxs[slot]], writes=[bxn[t], buf("ss%d" % t)])
            for t in range(2):
                S.add("pool", (lambda e, t=t: e.tensor_scalar(out=stat[:, 2 + t:3 + t], in0=stat[:, t:t + 1],
                                                              scalar1=EPS, scalar2=None, op0=ALU.add)),
                      reads=[buf("ss%d" % t)], writes=[buf("rv%d" % t)])
                S.add("pool", (lambda e, t=t: e.tensor_tensor(out=stat[:, 4 + t:5 + t], in0=stat[:, 2 + t:3 + t],
                                                              in1=mhalf[:], op=ALU.pow)),
                      reads=[buf("rv%d" % t), buf("mhalf")], writes=[buf("r%d" % t)])
            for t in range(2):
                slot = t
                S.add("act", (lambda e, slot=slot, t=t: e.activation(
                    out=xn[t][:], in_=xs[slot][:], func=AF.Identity, scale=stat[:, 4 + t:5 + t])),
                    reads=[bxs[slot], buf("r%d" % t)], writes=[bxn[t]])

        def phase1b(hslot):
            bA, bB = next_bank(), next_bank()
            vA = bank(bA).bitcast(BF16).rearrange("p (k t) -> p k t", k=4)
            vB = bank(bB).bitcast(BF16).rearrange("p (k t) -> p k t", k=4)
            for t in range(2):
                def tr(e, t=t):
                    ins = None
                    for k in range(8):
                        v = vA if k < 4 else vB
                        ins = e.transpose(v[:, k % 4, t * 128:(t + 1) * 128], xn[t][:, k * 128:(k + 1) * 128], ident[:])
                    return ins
                S.add("pe", tr, reads=[bxn[t], buf("ident")], writes=[bbank[bA], bbank[bB]])
            for half, (bi, v) in enumerate(((bA, vA), (bB, vB))):
                g_b = gcol[:, half * 4:(half + 1) * 4].unsqueeze(2).to_broadcast([128, 4, TB])
                S.add("dve", (lambda e, half=half, v=v, g_b=g_b: e.tensor_tensor(
                    out=hT[hslot][:, half * 4:(half + 1) * 4, :], in0=v, in1=g_b, op=ALU.mult)),
                    reads=[bbank[bi], buf("gcol")], writes=[bhT[hslot]])

        def proj_pair(hslot, col0, col1):
            bi = next_bank()

            def f(e):
                ins = None
                for half, c0 in enumerate((col0, col1)):
                    for k in range(8):
                        ins = e.matmul(bank(bi)[:, half * TB:(half + 1) * TB], lhsT=Win[:, k, c0:c0 + 128],
                                       rhs=hT[hslot][:, k, :], start=(k == 0), stop=(k == 7))
                return ins
            cgs = sorted(set((col0 // 512, col1 // 512)))
            S.add("pe", f, reads=[bhT[hslot]] + [bWcg[c] for c in cgs], writes=[bbank[bi]])
            return bi

        def chunks_kv(hslot, kb0):
            ring = kb0 % 8
            rs = ring // 2
            out = []
            for j in range(2):
                def ck(j=j):
                    bi = proj_pair(hslot, C_K + (2 * j) * 128, C_K + (2 * j + 1) * 128)
                    S.add("dve", (lambda e: e.tensor_copy(
                        out=kT[:, 2 * j:2 * j + 2, ring * 128:ring * 128 + TB],
                        in_=bank(bi).rearrange("p (a t) -> p a t", a=2))),
                        reads=[bbank[bi]], writes=[bkT[rs]])
                out.append(ck)
            for t in range(2):
                def cv(t=t):
                    bi = next_bank()

                    def f(e):
                        ins = None
                        for k in range(8):
                            ins = e.matmul(bank(bi), lhsT=hT[hslot][:, k, t * 128:(t + 1) * 128],
                                           rhs=Win[:, k, C_V:C_V + 512], start=(k == 0), stop=(k == 7))
                        return ins
                    S.add("pe", f, reads=[bhT[hslot], bWcg[2]], writes=[bbank[bi]])
                    S.add("act", (lambda e: e.activation(
                        out=Vaug[:, ring + t, :, 0:64], in_=bank(bi).rearrange("p (h d) -> p h d", h=8), func=AF.Copy)),
                        reads=[bbank[bi]], writes=[bV[rs]])
                out.append(cv)
            return out

        def phase2_qz(hslot):
            for j in range(2):
                bi = proj_pair(hslot, C_Q + (2 * j) * 128, C_Q + (2 * j + 1) * 128)
                for hh in range(2):
                    p0 = hh * 64
                    S.add("act", (lambda e, bi=bi, j=j, hh=hh, p0=p0: e.activation(
                        out=qTz[p0:p0 + 64, 2 * j:2 * j + 2, hh, :],
                        in_=bank(bi)[p0:p0 + 64, :].rearrange("p (a t) -> p a t", a=2), func=AF.Copy)),
                        reads=[bbank[bi]], writes=[bqTz])
            for j in range(2):
                bi = proj_pair(hslot, C_ZA + (2 * j) * 128, C_ZA + (2 * j + 1) * 128)
                S.add("act", (lambda e, bi=bi: e.activation(out=tz[:], in_=bank(bi), func=AF.Tanh, scale=0.5)),
                      reads=[bbank[bi]], writes=[btz])
                S.add("dve", (lambda e, bi=bi, j=j: e.scalar_tensor_tensor(
                    out=sz2[:, 2 * j:2 * j + 2, :], in0=tz[:].rearrange("p (a t) -> p a t", a=2), scalar=1.0,
                    in1=bank(bi).rearrange("p (a t) -> p a t", a=2), op0=ALU.add, op1=ALU.mult)),
                    reads=[bbank[bi], btz], writes=[buf("sz2")])

        def conv_hist_init(hslot):
            bi = next_bank()

            def f(e):
                ins = None
                for g, cbase in enumerate((C_C, C_U)):
                    for j in range(4):
                        for k in range(8):
                            o = g * 8 + j * 2
                            ins = e.matmul(bank(bi)[:, o:o + 2], lhsT=Win[:, k, cbase + j * 128:cbase + (j + 1) * 128],
                                           rhs=hT[hslot][:, k, TB - 2:TB], start=(k == 0), stop=(k == 7))
                return ins
            S.add("pe", f, reads=[bhT[hslot], bWcg[5], bWcg[6]], writes=[bbank[bi]])
            S.add("act", lambda e: e.activation(out=u_sb[:, 0:8], in_=bank(bi)[:, 8:16], func=AF.Copy),
                  reads=[bbank[bi]], writes=[buf("u_sb")])
            S.add("dve", lambda e: e.tensor_tensor(
                out=cu[:, :, TB:TB + 2], in0=bank(bi)[:, 0:8].rearrange("p (j t) -> p j t", j=4),
                in1=u_sb[:, 0:8].rearrange("p (j t) -> p j t", j=4), op=ALU.mult),
                reads=[bbank[bi], buf("u_sb")], writes=[buf("cu%d" % j) for j in range(4)])

        def chunks_conv(hslot, cslot):
            out = []
            for j in range(4):
                st = {}

                def cx(j=j, st=st):
                    bX = proj_pair(hslot, C_C + j * 128, C_U + j * 128)
                    st["bX"] = bX
                    bcu = buf("cu%d" % j)
                    S.add("act", (lambda e: e.activation(out=u_sb[:], in_=bank(bX)[:, TB:2 * TB], func=AF.Copy)),
                          reads=[bbank[bX]], writes=[buf("u_sb")])
                    S.add("dve", (lambda e: e.tensor_copy(out=cu[:, j, 0:2], in_=cu[:, j, TB:TB + 2])),
                          reads=[bcu], writes=[bcu])
                    S.add("dve", (lambda e: e.tensor_tensor(out=cu[:, j, 2:TB + 2], in0=bank(bX)[:, 0:TB],
                                                            in1=u_sb[:], op=ALU.mult)),
                          reads=[bbank[bX], buf("u_sb"), bcu], writes=[bcu])
                    S.add("act", (lambda e: e.activation(out=accb[:], in_=cu[:, j, 2:TB + 2], func=AF.Identity,
                                                         scale=cwt[:, j * 3 + 2:j * 3 + 3], bias=cb[:, j:j + 1])),
                          reads=[bcu, buf("cwt"), buf("cb")], writes=[buf("accb")])
                    S.add("dve", (lambda e: e.scalar_tensor_tensor(out=accb[:], in0=cu[:, j, 1:TB + 1],
                                                                   scalar=cwt[:, j * 3 + 1:j * 3 + 2], in1=accb[:],
                                                                   op0=ALU.mult, op1=ALU.add)),
                          reads=[bcu, buf("cwt"), buf("accb")], writes=[buf("accb")])
                    S.add("dve", (lambda e: e.scalar_tensor_tensor(out=accb[:], in0=cu[:, j, 0:TB],
                                                                   scalar=cwt[:, j * 3:j * 3 + 1], in1=accb[:],
                                                                   op0=ALU.mult, op1=ALU.add)),
                          reads=[bcu, buf("cwt"), buf("accb")], writes=[buf("accb")])
                out.append(cx)

                def cy(j=j, st=st):
                    bY = proj_pair(hslot, C_B + j * 128, C_ZC + j * 128)
                    S.add("act", (lambda e: e.activation(out=tzc[:], in_=bank(bY)[:, TB:2 * TB], func=AF.Tanh, scale=0.5)),
                          reads=[bbank[bY]], writes=[buf("tzc")])
                    S.add("dve", (lambda e: e.tensor_tensor(out=accb[:], in0=bank(bY)[:, 0:TB], in1=accb[:], op=ALU.mult)),
                          reads=[bbank[bY], buf("accb")], writes=[buf("accb")])
                    S.add("dve", (lambda e: e.scalar_tensor_tensor(out=tzc[:], in0=tzc[:], scalar=1.0,
                                                                   in1=bank(bY)[:, TB:2 * TB], op0=ALU.add, op1=ALU.mult)),
                          reads=[bbank[bY], buf("tzc")], writes=[buf("tzc")])
                    S.add("dve", (lambda e: e.tensor_tensor(out=convg[cslot][:, j, :], in0=accb[:], in1=tzc[:], op=ALU.mult)),
                          reads=[buf("accb"), buf("tzc")], writes=[bconvg[cslot]])
                out.append(cy)
            return out

        unit_ctr = [0]

        def phase4(blk, filler):
            units = [(2 * blk + qi, h) for qi in range(2) for h in range(NH)]
            base = unit_ctr[0]
            unit_ctr[0] += len(units)
            nfill = len(filler)
            fdone = 0

            def slots_kb(i):
                return [i - 3, i - 2, i - 4, i - 1, i]

            def rec_front(n, i, h):
                pt = n % 3
                A = 2 + 2 * (n % 3)
                sc = ps[:, A * 512:A * 512 + 640]
                jt, hv = h // 2, h % 2
                qcol = (i % 2) * 128
                kbs = slots_kb(i)

                def qk(e):
                    def mm_qk(s, **kw):
                        ring = kbs[s] % 8
                        return e.matmul(sc[:, s * 128:(s + 1) * 128], lhsT=kT[:, jt, ring * 128:(ring + 1) * 128],
                                        rhs=qTz[:, jt, hv, qcol:qcol + 128], **kw)
                    mm_qk(0, start=True, stop=True)
                    mm_qk(1, start=True, stop=True)
                    e.matmul(sc[:, 256:512], lhsT=ident[:], rhs=tab[:, h, 0:256], start=False, stop=False,
                             skip_group_check=True)
                    mm_qk(2, start=False, stop=False, skip_group_check=True)
                    mm_qk(3, start=False, stop=True, skip_group_check=True)
                    e.matmul(sc[:, 512:640], lhsT=ident[:], rhs=tab[:, h, 256:384], start=True, stop=False)
                    return mm_qk(4, start=False, stop=True)
                S.add("pe", qk, reads=[bqTz, buf("tab"), buf("ident")] + [bkT[(kb % 8) // 2] for kb in kbs],
                      writes=[bbank[A], bbank[A + 1]])
                runs = []
                for s, kb in enumerate(kbs):
                    hal = kb < 0
                    if runs and runs[-1][2] == hal:
                        runs[-1][1] = s + 1
                    else:
                        runs.append([s, s + 1, hal])
                for ri, (s0, s1, hal) in enumerate(runs):
                    if hal:
                        S.add("act", (lambda e, s0=s0, s1=s1, pt=pt, sc=sc: e.activation(
                            out=PT[pt][:, s0 * 128:s1 * 128], in_=sc[:, s0 * 128:s1 * 128], func=AF.Exp,
                            bias=hmask[:, 0:1], scale=0.125)),
                            reads=[bbank[A], bbank[A + 1], buf("hmask")], writes=[bPT[pt]])
                    else:
                        S.add("act", (lambda e, s0=s0, s1=s1, pt=pt, sc=sc: e.activation(
                            out=PT[pt][:, s0 * 128:s1 * 128], in_=sc[:, s0 * 128:s1 * 128], func=AF.Exp,
                            scale=0.125)),
                            reads=[bbank[A], bbank[A + 1]], writes=[bPT[pt]])

            def rec_back(n, i, h):
                pt = n % 3
                Bk = 3 + 2 * (n % 3)
                acc = ps[:, Bk * 512 + 128:Bk * 512 + 193]
                kbs = slots_kb(i)
                asl = i % 2

                def pv(e):
                    ins = None
                    for s, kb in enumerate(kbs):
                        ring = kb % 8
                        ins = e.matmul(acc, lhsT=PT[pt][:, s * 128:(s + 1) * 128], rhs=Vaug[:, ring, h, :],
                                       start=(s == 0), stop=(s == 4))
                    return ins
                S.add("pe", pv, reads=[bPT[pt]] + [bV[(kb % 8) // 2] for kb in kbs], writes=[bbank[Bk]])
                S.add("dve", (lambda e, h=h, acc=acc: e.reciprocal(out=rc[:, h:h + 1], in_=acc[:, 64:65])),
                      reads=[bbank[Bk]], writes=[buf("rc%d" % h)])
                S.add("dve", (lambda e, h=h, acc=acc, asl=asl: e.tensor_scalar(
                    out=att_tok[asl][:, h * 64:(h + 1) * 64], in0=acc[:, 0:64], scalar1=rc[:, h:h + 1], scalar2=None,
                    op0=ALU.mult)),
                    reads=[bbank[Bk], buf("rc%d" % h)], writes=[batt[asl]])
                if h == NH - 1:
                    trv = ps[:, Bk * 512 + 256:Bk * 512 + 512].bitcast(BF16).rearrange("p (j t) -> p j t", j=4)
                    qcol = (i % 2) * 128

                    def tr_and_gate():
                        def trf(e):
                            ins = None
                            for j in range(4):
                                ins = e.transpose(trv[:, j, :], att_tok[asl][:, j * 128:(j + 1) * 128], ident[:])
                            return ins
                        S.add("pe", trf, reads=[batt[asl], buf("ident")], writes=[bbank[Bk]])
                        S.add("dve", (lambda e: e.tensor_tensor(
                            out=attgT[:, :, qcol:qcol + 128], in0=trv, in1=sz2[:, :, qcol:qcol + 128], op=ALU.mult)),
                            reads=[bbank[Bk], buf("sz2")], writes=[buf("attgT")])
                    pending.append(tr_and_gate)

            bank_pool[0] = [0, 1]
            pending = []
            for k in range(min(P4_DEPTH, len(units))):
                rec_front(base + k, *units[k])
            for idx, (i, h) in enumerate(units):
                want = ((idx + 1) * nfill + len(units) - 1) // len(units)
                while fdone < min(want, nfill):
                    filler[fdone]()
                    fdone += 1
                rec_back(base + idx, i, h)
                if idx + P4_DEPTH < len(units):
                    rec_front(base + idx + P4_DEPTH, *units[idx + P4_DEPTH])
                if pending and h != NH - 1:
                    pending.pop(0)()
            while pending:
                pending.pop(0)()
            while fdone < nfill:
                filler[fdone]()
                fdone += 1
            bank_pool[0] = [0, 1, 2, 3]

        def phase5(hslot, cslot, filler=()):
            bank_pool[0] = [0, 1, 2, 3, 4, 5, 6, 7]
            filler = list(filler)
            for j in range(8):
                bG = proj_pair(hslot, C_GA + j * 128, C_GC + j * 128)
                bY = next_bank()

                def f(e, bY=bY, j=j):
                    ins = None
                    for half, (W, src) in enumerate(((Watt, attgT), (Wconv, convg[cslot]))):
                        for k in range(4):
                            ins = e.matmul(bank(bY)[:, half * TB:(half + 1) * TB], lhsT=W[:, k, j * 128:(j + 1) * 128],
                                           rhs=src[:, k, :], start=(k == 0), stop=(k == 3))
                    return ins
                S.add("pe", f, reads=[bWatt, bWconv, buf("attgT"), bconvg[cslot]], writes=[bbank[bY]])
                S.add("act", (lambda e, bG=bG: e.activation(out=tg[:], in_=bank(bG), func=AF.Tanh, scale=0.5)),
                      reads=[bbank[bG]], writes=[btz])
                S.add("dve", (lambda e, bY=bY: e.scalar_tensor_tensor(out=mm[:], in0=tg[:], scalar=1.0, in1=bank(bY),
                                                                      op0=ALU.add, op1=ALU.mult)),
                      reads=[bbank[bY], btz], writes=[buf("mm")])
                S.add("pool", (lambda e, j=j: e.tensor_tensor(out=mT[:, j, :], in0=mm[:, 0:TB], in1=mm[:, TB:2 * TB],
                                                              op=ALU.add)),
                      reads=[buf("mm")], writes=[buf("mT")])
                if j < len(filler):
                    filler[j]()
            for c in filler[8:]:
                c()
            bank_pool[0] = [0, 1, 2, 3]

        def phase6_load(blk):
            for t in range(2):
                slot = t
                r0 = blk * TB + t * 128
                S.add("sp", (lambda e, slot=slot, r0=r0: e.dma_start(out=xs[slot][:], in_=x_d[r0:r0 + 128, :])),
                      writes=[bxs[slot]], dma="xs%d" % slot)

        def phase6(blk):
            for t in range(2):
                slot = t
                r0 = blk * TB + t * 128
                for c in range(2):
                    bi = next_bank()

                    def f(e, bi=bi, t=t, c=c):
                        ins = None
                        for k in range(8):
                            ins = e.matmul(bank(bi), lhsT=mT[:, k, t * 128:(t + 1) * 128],
                                           rhs=Wout[:, k, c * 512:(c + 1) * 512], start=(k == 0), stop=(k == 7))
                        return ins
                    S.add("pe", f, reads=[buf("mT"), bWout], writes=[bbank[bi]])
                    S.add("dve", (lambda e, bi=bi, slot=slot, c=c: e.scalar_tensor_tensor(
                        out=xs[slot][:, c * 512:(c + 1) * 512], in0=bank(bi), scalar=0.25,
                        in1=xs[slot][:, c * 512:(c + 1) * 512], op0=ALU.mult, op1=ALU.add)),
                        reads=[bbank[bi], bxs[slot]], writes=[bxs[slot]])
            for t in range(2):
                slot = t
                r0 = blk * TB + t * 128
                junk, bjunk = ((mm, buf("mm")), (tg, btz))[t]
                S.add("act", (lambda e, slot=slot, t=t, junk=junk: e.activation(
                    out=junk[:].bitcast(BF16), in_=xs[slot][:], func=AF.Square, scale=1.0 / 32.0,
                    accum_out=stat[:, 6 + t:7 + t])),
                    reads=[bxs[slot]], writes=[bjunk, buf("fss%d" % t)])
                S.add("pool", (lambda e, t=t: e.tensor_scalar(out=stat[:, 6 + t:7 + t], in0=stat[:, 6 + t:7 + t],
                                                              scalar1=EPS, scalar2=None, op0=ALU.add)),
                      reads=[buf("fss%d" % t)], writes=[buf("fss%d" % t)])
                S.add("pool", (lambda e, t=t: e.tensor_tensor(out=stat[:, 6 + t:7 + t], in0=stat[:, 6 + t:7 + t],
                                                              in1=mhalf[:], op=ALU.pow)),
                      reads=[buf("fss%d" % t), buf("mhalf")], writes=[buf("fss%d" % t)])
                S.add("dve", (lambda e, slot=slot, t=t: e.scalar_tensor_tensor(
                    out=xs[slot][:], in0=xs[slot][:], scalar=stat[:, 6 + t:7 + t], in1=fg[:], op0=ALU.mult, op1=ALU.mult)),
                    reads=[bxs[slot], buf("fss%d" % t), buf("fg")], writes=[bxs[slot]])
                S.add("sp", (lambda e, slot=slot, r0=r0: e.dma_start(out=y_d[r0:r0 + 128, :], in_=xs[slot][:])),
                      reads=[bxs[slot]], dma="xs%d" % slot, is_out=True)

        def HS(b):
            return (b + 1) % 2

        S.tag = "setup"
        S.add("pool", lambda e: e.memset(mhalf[:], -0.5), writes=[buf("mhalf")])
        small(gcol[:], gcol_d, "gcol")
        small(tg[:, 0:128], ident_d, "identf", "tg")
        for cg in (1, 2, 0, 3):
            load_cg(cg)
        S.tag = "h0"
        phase1a(xh_d, 0)
        for cg in (5, 6, 4, 7):
            load_cg(cg)
        S.add("dve", lambda e: e.memset(Vaug[:, :, :, 64:65], 1.0), writes=bV)
        S.add("dve", lambda e: e.memset(qTz[64:128, :, 0, :], 0.0), writes=[bqTz])
        S.add("dve", lambda e: e.memset(qTz[0:64, :, 1, :], 0.0), writes=[bqTz])
        S.add("dve", lambda e: e.tensor_copy(out=ident[:], in_=tg[:, 0:128]), reads=[btz], writes=[buf("ident")])
        phase1b(0)
        S.tag = "b0.P1"
        phase1a(x_d, 0)
        load_w(Watt, w_att_d, bWatt, "watt")
        load_w(Wconv, w_conv_d, bWconv, "wconv")
        for cg in (8, 10, 9, 11):
            load_cg(cg)
        load_w(Wout, w_out_d, bWout, "wout")
        small(hmask[:], hmask_d, "hmask")
        small(cfar[:], cfar_d, "cfar")
        small(cwt[:], cwt_d, "cwt")
        small(cb[:], cb_d, "cb")
        S.tag = "h0"
        for c in chunks_kv(0, -4):
            c()
        S.tag = "b0.P1"
        phase1b(HS(0))
        S.tag = "setup"
        setup_tables()
        S.tag = "b0.P2"
        for c in chunks_kv(HS(0), 0):
            c()
        S.tag = "h1"
        phase1a(xh_d, TB)
        S.tag = "b0.P2"
        phase2_qz(HS(0))
        S.tag = "h1"
        phase1b(0)
        conv_hist_init(0)
        if nblk > 1:
            S.tag = "b1.P1"
            phase1a(x_d, TB)
        S.tag = "setup"
        small(fg[:], fg_d, "fg")
        S.tag = "b0.P3"
        for c in chunks_conv(HS(0), 0):
            c()
        S.tag = "h1"
        for c in chunks_kv(0, -2):
            c()
        if nblk > 1:
            S.tag = "b1.P1"
            phase1b(HS(1))
        for blk in range(nblk):
            hs = HS(blk)
            nb = blk + 1
            S.tag = "b%d.P4" % blk
            fill4, fill5 = [], []
            if nb < nblk:
                ccv = chunks_conv(HS(nb), nb % 2)
                fill4 = chunks_kv(HS(nb), 2 * nb) + ccv[:N_CONV_IN_P4]
                fill5 = ccv[N_CONV_IN_P4:]
            phase4(blk, fill4)
            if nb < nblk:
                S.tag = "b%d.P2q" % nb
                phase2_qz(HS(nb))
            if blk + 2 < nblk:
                S.tag = "b%d.P1" % (blk + 2)
                phase1a(x_d, (blk + 2) * TB)
            S.tag = "b%d.P6" % blk
            phase6_load(blk)
            S.tag = "b%d.P5" % blk
            phase5(hs, blk % 2, fill5)
            if blk + 2 < nblk:
                S.tag = "b%d.P1" % (blk + 2)
                phase1b(hs)
            S.tag = "b%d.P6" % blk
            phase6(blk)

        S.emit(nc, es)
    return nc


def _bias_index_table():
    p = np.arange(128)[:, None]
    jq = np.arange(128)[None, :]
    far = np.full((128, 128), 256, dtype=np.int64)
    far[(p < 64) & (jq >= 64)] = -1
    relA = 128 + jq - p
    nearA = np.minimum(relA, 128) + 128
    relB = jq - p
    nearB = relB + 128
    nearB = np.where((p >= 64) & (jq < 64), -1, nearB)
    return np.concatenate([far, nearA, nearB], axis=1)


def kernel(x, norm_g, w_in, rel_bias, w_att_out, conv_w, conv_b, w_conv_out, w_out, final_norm_g):
    x = np.asarray(x, dtype=np.float32)
    f32 = np.float32
    w_in0 = np.ascontiguousarray(np.asarray(w_in, f32)[0])
    w_att0 = np.ascontiguousarray(np.asarray(w_att_out, f32)[0])
    w_conv0 = np.ascontiguousarray(np.asarray(w_conv_out, f32)[0])
    w_out0 = np.ascontiguousarray(np.asarray(w_out, f32)[0])
    g = np.asarray(norm_g, f32)[0]
    fgv = np.asarray(final_norm_g, f32)
    rb = np.asarray(rel_bias, f32)[0]
    cw = np.asarray(conv_w, f32)[0]
    cbv = np.asarray(conv_b, f32)[0]

    fg_bc = np.ascontiguousarray(np.broadcast_to(fgv[None, :], (128, D)))
    gcol = np.ascontiguousarray(g.reshape(8, 128).T)
    cfar = np.ascontiguousarray(np.broadcast_to(rb[:, 256][None, :], (128, 8)))
    cwt = np.ascontiguousarray(cw.reshape(3, 4, 128).transpose(2, 1, 0).reshape(128, 12))
    cbt = np.ascontiguousarray(cbv.reshape(4, 128).T)
    ident = np.eye(128, dtype=f32)
    idx = _bias_index_table()
    gathered = rb[:, np.maximum(idx, 0)]
    btab = np.where(idx[None, :, :] >= 0, gathered, f32(NEG)).astype(f32)
    btab = np.ascontiguousarray(btab.transpose(1, 0, 2).reshape(128, 8 * 384))

    in_maps = []
    for c in range(N_CORES):
        b, seg = c // 4, c % 4
        t0 = seg * TOK
        xc = np.ascontiguousarray(x[b, t0:t0 + TOK, :])
        if seg == 0:
            xh = np.zeros((HALO, D), f32)
            hm = np.tile(np.array([[NEG, 0.0]], f32), (128, 1))
        else:
            xh = np.ascontiguousarray(x[b, t0 - HALO:t0, :])
            hm = np.tile(np.array([[0.0, 1.0]], f32), (128, 1))
        in_maps.append({
            "x": xc, "xh": xh, "w_in": w_in0, "w_att": w_att0, "w_conv": w_conv0, "w_out": w_out0,
            "fg": fg_bc, "gcol": gcol, "cfar": cfar, "hmask": np.ascontiguousarray(hm), "cwt": cwt, "cb": cbt,
            "ident": ident, "btab": btab,
        })
    nc = build_program()
    res = run_bass_kernel_spmd(nc, in_maps, core_ids=list(range(N_CORES)))
    out = np.empty((NB, SEQ, D), dtype=np.float32)
    for c in range(N_CORES):
        b, seg = c // 4, c % 4
        out[b, seg * TOK:(seg + 1) * TOK, :] = res.results[c]["y"]
    return out
```

```python
import numpy as np
from contextlib import ExitStack

import concourse.bass as bass
import concourse.mybir as mybir
from concourse.bass_utils import run_bass_kernel_spmd

F32 = mybir.dt.float32
BF16 = mybir.dt.bfloat16
AF = mybir.ActivationFunctionType
ALU = mybir.AluOpType

D = 1024
SEQ = 8192
NB = 2
TOK = 2048
HALO = 512
TB = 256
NBLK = TOK // TB
NH = 8
EPS = 1e-6
NEG = -30000.0
N_CORES = 8

C_Q, C_K, C_V, C_ZA, C_B, C_C, C_U, C_ZC, C_GA, C_GC = 0, 512, 1024, 1536, 2048, 2560, 3072, 3584, 4096, 5120


class Buf:
    __slots__ = ("name", "w", "rs", "excl")

    def __init__(self, name, excl=False):
        self.name = name
        self.w = None
        self.rs = []
        self.excl = excl


class Op:
    __slots__ = ("eng", "fn", "deps", "dma", "sig", "tick", "sem", "idx", "tag")

    def __init__(self, eng, fn, dma):
        self.eng = eng
        self.fn = fn
        self.dma = dma
        self.deps = []
        self.sig = False
        self.tick = 0
        self.sem = None
        self.idx = 0


ENGS = ("pe", "act", "dve", "pool", "sp")
ANNOTATE = False

FILL_MODE = "all"
P4_DEPTH = 2
N_CONV_IN_P4 = 0
TAB_ENG = "dve"


class Sched:
    def __init__(self):
        self.q = {e: [] for e in ENGS}
        self.n = 0
        self.out_keys = set()
        self.tag = ""

    def add(self, eng, fn, reads=(), writes=(), dma=None, is_out=False):
        op = Op(eng, fn, dma)
        op.idx = self.n
        op.tag = self.tag
        self.n += 1
        deps = []
        for b in reads:
            if b.w is not None:
                deps.append((b.w, "raw"))
            if b.excl:
                for r in b.rs:
                    if r.eng != eng:
                        deps.append((r, "xrd"))
        for b in writes:
            if b.w is not None:
                deps.append((b.w, "waw"))
            for r in b.rs:
                deps.append((r, "war"))
        for b in writes:
            b.w = op
            b.rs = []
        wset = set(id(b) for b in writes)
        for b in reads:
            if id(b) not in wset:
                b.rs.append(op)
        seen = set()
        for d, kind in deps:
            if d is op:
                continue
            need = True
            if d.dma is None and op.dma is None and d.eng == op.eng:
                if op.eng == "pe":
                    need = False
                elif kind != "raw":
                    need = False
            if not need:
                continue
            if id(d) in seen:
                continue
            seen.add(id(d))
            op.deps.append(d)
            d.sig = True
        if dma is not None and is_out:
            self.out_keys.add(dma)
        self.q[eng].append(op)
        return op

    def emit(self, nc, es):
        esem = {e: es.enter_context(nc.semaphore("sem_" + e)) for e in ENGS}
        dkeys = []
        for e in ENGS:
            for op in self.q[e]:
                if op.dma is not None and op.dma not in dkeys:
                    dkeys.append(op.dma)
        dsem = {k: es.enter_context(nc.semaphore("dsem_" + k)) for k in dkeys}
        dcnt = {k: 0 for k in dkeys}
        allops = []
        for e in ENGS:
            allops.extend(self.q[e])
        allops.sort(key=lambda o: o.idx)
        ecnt = {e: 0 for e in ENGS}
        for op in allops:
            if op.dma is not None:
                dcnt[op.dma] += 16
                op.tick = dcnt[op.dma]
                op.sem = dsem[op.dma]
            elif op.sig:
                ecnt[op.eng] += 1
                op.tick = ecnt[op.eng]
                op.sem = esem[op.eng]
        block = es.enter_context(nc.Block())
        out_final = [(dsem[k], dcnt[k]) for k in sorted(self.out_keys)]

        def run(eng_obj, ops, final=False):
            waited = {}
            for op in ops:
                for d in op.deps:
                    key = id(d.sem)
                    if waited.get(key, 0) < d.tick:
                        eng_obj.wait_ge(d.sem, d.tick)
                        waited[key] = d.tick
                ins = op.fn(eng_obj)
                if ANNOTATE and op.tag:
                    ins.annotate(op.tag)
                if op.dma is not None:
                    ins.then_inc(op.sem, 16)
                elif op.sig:
                    ins.then_inc(op.sem, 1)
            if final:
                for s, v in out_final:
                    eng_obj.wait_ge(s, v)

        q = self.q

        @block.tensor
        def _(eng):
            run(eng, q["pe"])

        @block.scalar
        def _(eng):
            run(eng, q["act"])

        @block.vector
        def _(eng):
            run(eng, q["dve"])

        @block.gpsimd
        def _(eng):
            run(eng, q["pool"])

        @block.sync
        def _(eng):
            run(eng, q["sp"], final=True)


def build_program(nblk=NBLK):
    nc = bass.Bass("TRN2", target_bir_lowering=False)

    def din(name, shape):
        return nc.dram_tensor(name, shape, F32, kind="ExternalInput").ap()

    x_d = din("x", [TOK, D])
    xh_d = din("xh", [HALO, D])
    w_in_d = din("w_in", [D, 6144])
    w_att_d = din("w_att", [512, D])
    w_conv_d = din("w_conv", [512, D])
    w_out_d = din("w_out", [D, D])
    fg_d = din("fg", [128, D])
    gcol_d = din("gcol", [128, 8])
    cfar_d = din("cfar", [128, 8])
    hmask_d = din("hmask", [128, 2])
    cwt_d = din("cwt", [128, 12])
    cb_d = din("cb", [128, 4])
    ident_d = din("ident", [128, 128])
    btab_d = din("btab", [128, 8 * 384])
    y_d = nc.dram_tensor("y", [TOK, D], F32, kind="ExternalOutput").ap()

    S = Sched()
    with ExitStack() as es:
        def sb(name, shape, dt):
            return es.enter_context(nc.sbuf_tensor("sb_" + name, shape, dt))

        Win = sb("Win", [128, 8, 6144], BF16)
        Watt = sb("Watt", [128, 4, 1024], BF16)
        Wconv = sb("Wconv", [128, 4, 1024], BF16)
        Wout = sb("Wout", [128, 8, 1024], BF16)
        fg = sb("fg", [128, D], F32)
        gcol = sb("gcol", [128, 8], F32)
        cfar = sb("cfar", [128, 8], F32)
        negc = sb("negc", [128, 8], F32)
        hmask = sb("hmask", [128, 2], F32)
        cwt = sb("cwt", [128, 12], F32)
        cb = sb("cb", [128, 4], F32)
        ident = sb("ident", [128, 128], BF16)
        tab = sb("tab", [128, 8, 384], BF16)
        mhalf = sb("mhalf", [128, 1], F32)
        xs = [sb("xs%d" % i, [128, D], F32) for i in range(2)]
        xn = [sb("xn%d" % i, [128, D], BF16) for i in range(2)]
        hT = [sb("hT%d" % i, [128, 8, TB], BF16) for i in range(2)]
        qTz = sb("qTz", [128, 4, 2, TB], BF16)
        kT = sb("kT", [128, 4, 1024], BF16)
        Vaug = sb("Vaug", [128, 8, 8, 65], BF16)
        sz2 = sb("sz2", [128, 4, TB], BF16)
        u_sb = sb("u_sb", [128, TB], F32)
        cu = sb("cu", [128, 4, TB + 2], F32)
        accb = sb("accb", [128, TB], F32)
        tzc = sb("tzc", [128, TB], F32)
        convg = [sb("convg%d" % i, [128, 4, TB], BF16) for i in range(2)]
        PT = [sb("PT%d" % i, [128, 640], BF16) for i in range(3)]
        att_tok = [sb("att_tok%d" % i, [128, 512], BF16) for i in range(2)]
        attgT = sb("attgT", [128, 4, TB], BF16)
        rc = sb("rc", [128, 8], F32)
        tg = sb("tg", [128, 512], F32)
        tz = tg
        mm = sb("mm", [128, 512], F32)
        mT = sb("mT", [128, 8, TB], BF16)
        stat = sb("stat", [128, 8], F32)
        ps = es.enter_context(nc.psum_tensor("ps", [128, 4096], F32))

        B = {}

        def buf(name):
            if name not in B:
                B[name] = Buf(name)
            return B[name]

        bWcg = [buf("Wcg%d" % i) for i in range(12)]
        bWatt, bWconv, bWout = buf("Watt"), buf("Wconv"), buf("Wout")
        bxs = [buf("xs0"), buf("xs1")]
        bxn = [buf("xn0"), buf("xn1")]
        bhT = [buf("hT0"), buf("hT1")]
        bqTz = buf("qTz")
        bkT = [buf("kT%d" % i) for i in range(4)]
        bV = [buf("V%d" % i) for i in range(4)]
        bbank = [buf("bank%d" % i) for i in range(8)]
        for b_ in bbank:
            b_.excl = True
        bPT = [buf("PT%d" % i) for i in range(3)]
        batt = [buf("att0"), buf("att1")]
        bconvg = [buf("convg0"), buf("convg1")]
        btz = buf("tg")

        def bank(i):
            return ps[:, i * 512:(i + 1) * 512]

        bank_pool = [[0, 1, 2, 3]]
        gb_ctr = [0]

        def next_bank():
            pool = bank_pool[0]
            i = pool[gb_ctr[0] % len(pool)]
            gb_ctr[0] += 1
            return i

        w_in_v = w_in_d.rearrange("(k p) c -> p k c", p=128)

        def load_cg(cg):
            S.add("pool", lambda e: e.dma_start(out=Win[:, :, cg * 512:(cg + 1) * 512],
                                                in_=w_in_v[:, :, cg * 512:(cg + 1) * 512]),
                  writes=[bWcg[cg]], dma="w%d" % cg)

        def load_w(W, src, b, key):
            S.add("pool", lambda e: e.dma_start(out=W[:], in_=src.rearrange("(k p) c -> p k c", p=128)),
                  writes=[b], dma=key)

        def small(dst, src, name, bufname=None):
            S.add("sp", lambda e: e.dma_start(out=dst, in_=src), writes=[buf(bufname or name)], dma="c_" + name)

        def setup_tables():
            S.add("dve", lambda e: e.tensor_scalar(out=negc[:], in0=cfar[:], scalar1=-1.0, scalar2=None, op0=ALU.mult),
                  reads=[buf("cfar")], writes=[buf("negc")])
            mT32 = mT[:].rearrange("p a b -> p (a b)").bitcast(F32)
            stage = [(mT32[:, 0:384], buf("mT"), "tb0"), (mT32[:, 384:768], buf("mT"), "tb1"),
                     (mm[:, 0:384], buf("mm"), "tb2"),
                     (convg[0][:].rearrange("p a b -> p (a b)").bitcast(F32)[:, 0:384], bconvg[0], "tb3"),
                     (convg[1][:].rearrange("p a b -> p (a b)").bitcast(F32)[:, 0:384], bconvg[1], "tb4"),
                     (attgT[:].rearrange("p a b -> p (a b)").bitcast(F32)[:, 0:384], buf("attgT"), "tb5")]
            order = [2, 3, 4, 5, 0, 1, 2, 3]
            for h in range(8):
                view, b_, key = stage[order[h]]
                S.add("sp", (lambda e, h=h, view=view: e.dma_start(out=view, in_=btab_d[:, h * 384:(h + 1) * 384])),
                      writes=[b_], dma=key)
                S.add("dve", (lambda e, h=h, view=view: e.tensor_scalar(
                    out=tab[:, h, :], in0=view, scalar1=negc[:, h:h + 1], scalar2=8.0, op0=ALU.add, op1=ALU.mult)),
                    reads=[b_, buf("negc")], writes=[buf("tab")])

        def phase1a(src_d, row0):
            for t in range(2):
                slot = t
                r0 = row0 + t * 128
                S.add("sp", (lambda e, slot=slot, r0=r0: e.dma_start(out=xs[slot][:], in_=src_d[r0:r0 + 128, :])),
                      writes=[bxs[slot]], dma="xs%d" % slot)
            for t in range(2):
                slot = t
                S.add("act", (lambda e, slot=slot, t=t: e.activation(
                    out=xn[t][:], in_=xs[slot][:], func=AF.Square, scale=1.0 / 32.0, accum_out=stat[:, t:t + 1])),
                    reads=[bxs[slot]], writes=[bxn[t], buf("ss%d" % t)])
            for t in range(2):
                S.add("pool", (lambda e, t=t: e.tensor_scalar(out=stat[:, 2 + t:3 + t], in0=stat[:, t:t + 1],
                                                              scalar1=EPS, scalar2=None, op0=ALU.add)),
                      reads=[buf("ss%d" % t)], writes=[buf("rv%d" % t)])
                S.add("pool", (lambda e, t=t: e.tensor_tensor(out=stat[:, 4 + t:5 + t], in0=stat[:, 2 + t:3 + t],
                                                              in1=mhalf[:], op=ALU.pow)),
                      reads=[buf("rv%d" % t), buf("mhalf")], writes=[buf("r%d" % t)])
            for t in range(2):
                slot = t
                S.add("act", (lambda e, slot=slot, t=t: e.activation(
                    out=xn[t][:], in_=xs[slot][:], func=AF.Identity, scale=stat[:, 4 + t:5 + t])),
                    reads=[bxs[slot], buf("r%d" % t)], writes=[bxn[t]])

        def phase1b(hslot):
            bA, bB = next_bank(), next_bank()
            vA = bank(bA).bitcast(BF16).rearrange("p (k t) -> p k t", k=4)
            vB = bank(bB).bitcast(BF16).rearrange("p (k t) -> p k t", k=4)
            for t in range(2):
                def tr(e, t=t):
                    ins = None
                    for k in range(8):
                        v = vA if k < 4 else vB
                        ins = e.transpose(v[:, k % 4, t * 128:(t + 1) * 128], xn[t][:, k * 128:(k + 1) * 128], ident[:])
                    return ins
                S.add("pe", tr, reads=[bxn[t], buf("ident")], writes=[bbank[bA], bbank[bB]])
            for half, (bi, v) in enumerate(((bA, vA), (bB, vB))):
                g_b = gcol[:, half * 4:(half + 1) * 4].unsqueeze(2).to_broadcast([128, 4, TB])
                S.add("dve", (lambda e, half=half, v=v, g_b=g_b: e.tensor_tensor(
                    out=hT[hslot][:, half * 4:(half + 1) * 4, :], in0=v, in1=g_b, op=ALU.mult)),
                    reads=[bbank[bi], buf("gcol")], writes=[bhT[hslot]])

        def proj_pair(hslot, col0, col1):
            bi = next_bank()

            def f(e):
                ins = None
                for half, c0 in enumerate((col0, col1)):
                    for k in range(8):
                        ins = e.matmul(bank(bi)[:, half * TB:(half + 1) * TB], lhsT=Win[:, k, c0:c0 + 128],
                                       rhs=hT[hslot][:, k, :], start=(k == 0), stop=(k == 7))
                return ins
            cgs = sorted(set((col0 // 512, col1 // 512)))
            S.add("pe", f, reads=[bhT[hslot]] + [bWcg[c] for c in cgs], writes=[bbank[bi]])
            return bi

        def chunks_kv(hslot, kb0):
            ring = kb0 % 8
            rs = ring // 2
            out = []
            for j in range(2):
                def ck(j=j):
                    bi = proj_pair(hslot, C_K + (2 * j) * 128, C_K + (2 * j + 1) * 128)
                    S.add("dve", (lambda e: e.tensor_copy(
                        out=kT[:, 2 * j:2 * j + 2, ring * 128:ring * 128 + TB],
                        in_=bank(bi).rearrange("p (a t) -> p a t", a=2))),
                        reads=[bbank[bi]], writes=[bkT[rs]])
                out.append(ck)
            for t in range(2):
                def cv(t=t):
                    bi = next_bank()

                    def f(e):
                        ins = None
                        for k in range(8):
                            ins = e.matmul(bank(bi), lhsT=hT[hslot][:, k, t * 128:(t + 1) * 128],
                                           rhs=Win[:, k, C_V:C_V + 512], start=(k == 0), stop=(k == 7))
                        return ins
                    S.add("pe", f, reads=[bhT[hslot], bWcg[2]], writes=[bbank[bi]])
                    S.add("act", (lambda e: e.activation(
                        out=Vaug[:, ring + t, :, 0:64], in_=bank(bi).rearrange("p (h d) -> p h d", h=8), func=AF.Copy)),
                        reads=[bbank[bi]], writes=[bV[rs]])
                out.append(cv)
            return out

        def phase2_qz(hslot):
            for j in range(2):
                bi = proj_pair(hslot, C_Q + (2 * j) * 128, C_Q + (2 * j + 1) * 128)
                for hh in range(2):
                    p0 = hh * 64
                    S.add("act", (lambda e, bi=bi, j=j, hh=hh, p0=p0: e.activation(
                        out=qTz[p0:p0 + 64, 2 * j:2 * j + 2, hh, :],
                        in_=bank(bi)[p0:p0 + 64, :].rearrange("p (a t) -> p a t", a=2), func=AF.Copy)),
                        reads=[bbank[bi]], writes=[bqTz])
            for j in range(2):
                bi = proj_pair(hslot, C_ZA + (2 * j) * 128, C_ZA + (2 * j + 1) * 128)
                S.add("act", (lambda e, bi=bi: e.activation(out=tz[:], in_=bank(bi), func=AF.Tanh, scale=0.5)),
                      reads=[bbank[bi]], writes=[btz])
                S.add("dve", (lambda e, bi=bi, j=j: e.scalar_tensor_tensor(
                    out=sz2[:, 2 * j:2 * j + 2, :], in0=tz[:].rearrange("p (a t) -> p a t", a=2), scalar=1.0,
                    in1=bank(bi).rearrange("p (a t) -> p a t", a=2), op0=ALU.add, op1=ALU.mult)),
                    reads=[bbank[bi], btz], writes=[buf("sz2")])

        def conv_hist_init(hslot):
            bi = next_bank()

            def f(e):
                ins = None
                for g, cbase in enumerate((C_C, C_U)):
                    for j in range(4):
                        for k in range(8):
                            o = g * 8 + j * 2
                            ins = e.matmul(bank(bi)[:, o:o + 2], lhsT=Win[:, k, cbase + j * 128:cbase + (j + 1) * 128],
                                           rhs=hT[hslot][:, k, TB - 2:TB], start=(k == 0), stop=(k == 7))
                return ins
            S.add("pe", f, reads=[bhT[hslot], bWcg[5], bWcg[6]], writes=[bbank[bi]])
            S.add("act", lambda e: e.activation(out=u_sb[:, 0:8], in_=bank(bi)[:, 8:16], func=AF.Copy),
                  reads=[bbank[bi]], writes=[buf("u_sb")])
            S.add("dve", lambda e: e.tensor_tensor(
                out=cu[:, :, TB:TB + 2], in0=bank(bi)[:, 0:8].rearrange("p (j t) -> p j t", j=4),
                in1=u_sb[:, 0:8].rearrange("p (j t) -> p j t", j=4), op=ALU.mult),
                reads=[bbank[bi], buf("u_sb")], writes=[buf("cu%d" % j) for j in range(4)])

        def chunks_conv(hslot, cslot):
            out = []
            for j in range(4):
                st = {}

                def cx(j=j, st=st):
                    bX = proj_pair(hslot, C_C + j * 128, C_U + j * 128)
                    st["bX"] = bX
                    bcu = buf("cu%d" % j)
                    S.add("act", (lambda e: e.activation(out=u_sb[:], in_=bank(bX)[:, TB:2 * TB], func=AF.Copy)),
                          reads=[bbank[bX]], writes=[buf("u_sb")])
                    S.add("dve", (lambda e: e.tensor_copy(out=cu[:, j, 0:2], in_=cu[:, j, TB:TB + 2])),
                          reads=[bcu], writes=[bcu])
                    S.add("dve", (lambda e: e.tensor_tensor(out=cu[:, j, 2:TB + 2], in0=bank(bX)[:, 0:TB],
                                                            in1=u_sb[:], op=ALU.mult)),
                          reads=[bbank[bX], buf("u_sb"), bcu], writes=[bcu])
                    S.add("act", (lambda e: e.activation(out=accb[:], in_=cu[:, j, 2:TB + 2], func=AF.Identity,
                                                         scale=cwt[:, j * 3 + 2:j * 3 + 3], bias=cb[:, j:j + 1])),
                          reads=[bcu, buf("cwt"), buf("cb")], writes=[buf("accb")])
                    S.add("dve", (lambda e: e.scalar_tensor_tensor(out=accb[:], in0=cu[:, j, 1:TB + 1],
                                                                   scalar=cwt[:, j * 3 + 1:j * 3 + 2], in1=accb[:],
                                                                   op0=ALU.mult, op1=ALU.add)),
                          reads=[bcu, buf("cwt"), buf("accb")], writes=[buf("accb")])
                    S.add("dve", (lambda e: e.scalar_tensor_tensor(out=accb[:], in0=cu[:, j, 0:TB],
                                                                   scalar=cwt[:, j * 3:j * 3 + 1], in1=accb[:],
                                                                   op0=ALU.mult, op1=ALU.add)),
                          reads=[bcu, buf("cwt"), buf("accb")], writes=[buf("accb")])
                out.append(cx)

                def cy(j=j, st=st):
                    bY = proj_pair(hslot, C_B + j * 128, C_ZC + j * 128)
                    S.add("act", (lambda e: e.activation(out=tzc[:], in_=bank(bY)[:, TB:2 * TB], func=AF.Tanh, scale=0.5)),
                          reads=[bbank[bY]], writes=[buf("tzc")])
                    S.add("dve", (lambda e: e.tensor_tensor(out=accb[:], in0=bank(bY)[:, 0:TB], in1=accb[:], op=ALU.mult)),
                          reads=[bbank[bY], buf("accb")], writes=[buf("accb")])
                    S.add("dve", (lambda e: e.scalar_tensor_tensor(out=tzc[:], in0=tzc[:], scalar=1.0,
                                                                   in1=bank(bY)[:, TB:2 * TB], op0=ALU.add, op1=ALU.mult)),
                          reads=[bbank[bY], buf("tzc")], writes=[buf("tzc")])
                    S.add("dve", (lambda e: e.tensor_tensor(out=convg[cslot][:, j, :], in0=accb[:], in1=tzc[:], op=ALU.mult)),
                          reads=[buf("accb"), buf("tzc")], writes=[bconvg[cslot]])
                out.append(cy)
            return out

        unit_ctr = [0]

        def phase4(blk, filler):
            units = [(2 * blk + qi, h) for qi in range(2) for h in range(NH)]
            base = unit_ctr[0]
            unit_ctr[0] += len(units)
            nfill = len(filler)
            fdone = 0

            def slots_kb(i):
                return [i - 3, i - 2, i - 4, i - 1, i]

            def rec_front(n, i, h):
                pt = n % 3
                A = 2 + 2 * (n % 3)
                sc = ps[:, A * 512:A * 512 + 640]
                jt, hv = h // 2, h % 2
                qcol = (i % 2) * 128
                kbs = slots_kb(i)

                def mm_qk(e, s, **kw):
                    ring = kbs[s] % 8
                    return e.matmul(sc[:, s * 128:(s + 1) * 128], lhsT=kT[:, jt, ring * 128:(ring + 1) * 128],
                                    rhs=qTz[:, jt, hv, qcol:qcol + 128], **kw)

                def qkA(e):
                    mm_qk(e, 0, start=True, stop=True)
                    mm_qk(e, 1, start=True, stop=True)
                    e.matmul(sc[:, 256:512], lhsT=ident[:], rhs=tab[:, h, 0:256], start=False, stop=False,
                             skip_group_check=True)
                    mm_qk(e, 2, start=False, stop=False, skip_group_check=True)
                    return mm_qk(e, 3, start=False, stop=True, skip_group_check=True)

                def qkB(e):
                    e.matmul(sc[:, 512:640], lhsT=ident[:], rhs=tab[:, h, 256:384], start=True, stop=False)
                    return mm_qk(e, 4, start=False, stop=True)
                rd = [bqTz, buf("tab"), buf("ident")]
                S.add("pe", qkA, reads=rd + [bkT[(kb % 8) // 2] for kb in kbs[0:4]], writes=[bbank[A]])
                S.add("pe", qkB, reads=rd + [bkT[(kbs[4] % 8) // 2]], writes=[bbank[A + 1]])
                runs = []
                for s, kb in enumerate(kbs):
                    hal = kb < 0
                    if runs and runs[-1][2] == hal:
                        runs[-1][1] = s + 1
                    else:
                        runs.append([s, s + 1, hal])
                for ri, (s0, s1, hal) in enumerate(runs):
                    if hal:
                        S.add("act", (lambda e, s0=s0, s1=s1, pt=pt, sc=sc: e.activation(
                            out=PT[pt][:, s0 * 128:s1 * 128], in_=sc[:, s0 * 128:s1 * 128], func=AF.Exp,
                            bias=hmask[:, 0:1], scale=0.125)),
                            reads=[bbank[A], bbank[A + 1], buf("hmask")], writes=[bPT[pt]])
                    else:
                        S.add("act", (lambda e, s0=s0, s1=s1, pt=pt, sc=sc: e.activation(
                            out=PT[pt][:, s0 * 128:s1 * 128], in_=sc[:, s0 * 128:s1 * 128], func=AF.Exp,
                            scale=0.125)),
                            reads=[bbank[A], bbank[A + 1]], writes=[bPT[pt]])

            def rec_back(n, i, h):
                pt = n % 3
                Bk = 3 + 2 * (n % 3)
                acc = ps[:, Bk * 512 + 128:Bk * 512 + 193]
                kbs = slots_kb(i)
                asl = i % 2

                def pv(e):
                    ins = None
                    for s, kb in enumerate(kbs):
                        ring = kb % 8
                        ins = e.matmul(acc, lhsT=PT[pt][:, s * 128:(s + 1) * 128], rhs=Vaug[:, ring, h, :],
                                       start=(s == 0), stop=(s == 4))
                    return ins
                S.add("pe", pv, reads=[bPT[pt]] + [bV[(kb % 8) // 2] for kb in kbs], writes=[bbank[Bk]])
                S.add("dve", (lambda e, h=h, acc=acc: e.reciprocal(out=rc[:, h:h + 1], in_=acc[:, 64:65])),
                      reads=[bbank[Bk]], writes=[buf("rc%d" % h)])
                S.add("dve", (lambda e, h=h, acc=acc, asl=asl: e.tensor_scalar(
                    out=att_tok[asl][:, h * 64:(h + 1) * 64], in0=acc[:, 0:64], scalar1=rc[:, h:h + 1], scalar2=None,
                    op0=ALU.mult)),
                    reads=[bbank[Bk], buf("rc%d" % h)], writes=[batt[asl]])
                if h == NH - 1:
                    trv = ps[:, Bk * 512 + 256:Bk * 512 + 512].bitcast(BF16).rearrange("p (j t) -> p j t", j=4)
                    qcol = (i % 2) * 128

                    def tr_and_gate():
                        def trf(e):
                            ins = None
                            for j in range(4):
                                ins = e.transpose(trv[:, j, :], att_tok[asl][:, j * 128:(j + 1) * 128], ident[:])
                            return ins
                        S.add("pe", trf, reads=[batt[asl], buf("ident")], writes=[bbank[Bk]])
                        S.add("dve", (lambda e: e.tensor_tensor(
                            out=attgT[:, :, qcol:qcol + 128], in0=trv, in1=sz2[:, :, qcol:qcol + 128], op=ALU.mult)),
                            reads=[bbank[Bk], buf("sz2")], writes=[buf("attgT")])
                    pending.append(tr_and_gate)

            bank_pool[0] = [0, 1]
            pending = []
            for k in range(min(P4_DEPTH, len(units))):
                rec_front(base + k, *units[k])
            for idx, (i, h) in enumerate(units):
                want = ((idx + 1) * nfill + len(units) - 1) // len(units)
                while fdone < min(want, nfill):
                    filler[fdone]()
                    fdone += 1
                rec_back(base + idx, i, h)
                if idx + P4_DEPTH < len(units):
                    rec_front(base + idx + P4_DEPTH, *units[idx + P4_DEPTH])
                if pending and h != NH - 1:
                    pending.pop(0)()
            while pending:
                pending.pop(0)()
            while fdone < nfill:
                filler[fdone]()
                fdone += 1
            bank_pool[0] = [0, 1, 2, 3]

        def phase5(hslot, cslot, filler=()):
            bank_pool[0] = [0, 1, 2, 3, 4, 5, 6, 7]
            filler = list(filler)
            for j in range(8):
                bG = proj_pair(hslot, C_GA + j * 128, C_GC + j * 128)
                bY = next_bank()

                def f(e, bY=bY, j=j):
                    ins = None
                    for half, (W, src) in enumerate(((Watt, attgT), (Wconv, convg[cslot]))):
                        for k in range(4):
                            ins = e.matmul(bank(bY)[:, half * TB:(half + 1) * TB], lhsT=W[:, k, j * 128:(j + 1) * 128],
                                           rhs=src[:, k, :], start=(k == 0), stop=(k == 3))
                    return ins
                S.add("pe", f, reads=[bWatt, bWconv, buf("attgT"), bconvg[cslot]], writes=[bbank[bY]])
                S.add("act", (lambda e, bG=bG: e.activation(out=tg[:], in_=bank(bG), func=AF.Tanh, scale=0.5)),
                      reads=[bbank[bG]], writes=[btz])
                S.add("dve", (lambda e, bY=bY: e.scalar_tensor_tensor(out=mm[:], in0=tg[:], scalar=1.0, in1=bank(bY),
                                                                      op0=ALU.add, op1=ALU.mult)),
                      reads=[bbank[bY], btz], writes=[buf("mm")])
                S.add("pool", (lambda e, j=j: e.tensor_tensor(out=mT[:, j, :], in0=mm[:, 0:TB], in1=mm[:, TB:2 * TB],
                                                              op=ALU.add)),
                      reads=[buf("mm")], writes=[buf("mT")])
                if j < len(filler):
                    filler[j]()
            for c in filler[8:]:
                c()
            bank_pool[0] = [0, 1, 2, 3]

        def phase6_load(blk):
            for t in range(2):
                slot = t
                r0 = blk * TB + t * 128
                S.add("sp", (lambda e, slot=slot, r0=r0: e.dma_start(out=xs[slot][:], in_=x_d[r0:r0 + 128, :])),
                      writes=[bxs[slot]], dma="xs%d" % slot)

        def phase6(blk):
            for t in range(2):
                slot = t
                r0 = blk * TB + t * 128
                for c in range(2):
                    bi = next_bank()

                    def f(e, bi=bi, t=t, c=c):
                        ins = None
                        for k in range(8):
                            ins = e.matmul(bank(bi), lhsT=mT[:, k, t * 128:(t + 1) * 128],
                                           rhs=Wout[:, k, c * 512:(c + 1) * 512], start=(k == 0), stop=(k == 7))
                        return ins
                    S.add("pe", f, reads=[buf("mT"), bWout], writes=[bbank[bi]])
                    S.add("dve", (lambda e, bi=bi, slot=slot, c=c: e.scalar_tensor_tensor(
                        out=xs[slot][:, c * 512:(c + 1) * 512], in0=bank(bi), scalar=0.25,
                        in1=xs[slot][:, c * 512:(c + 1) * 512], op0=ALU.mult, op1=ALU.add)),
                        reads=[bbank[bi], bxs[slot]], writes=[bxs[slot]])
            for t in range(2):
                slot = t
                r0 = blk * TB + t * 128
                junk, bjunk = ((mm, buf("mm")), (tg, btz))[t]
                S.add("act", (lambda e, slot=slot, t=t, junk=junk: e.activation(
                    out=junk[:].bitcast(BF16), in_=xs[slot][:], func=AF.Square, scale=1.0 / 32.0,
                    accum_out=stat[:, 6 + t:7 + t])),
                    reads=[bxs[slot]], writes=[bjunk, buf("fss%d" % t)])
                S.add("pool", (lambda e, t=t: e.tensor_scalar(out=stat[:, 6 + t:7 + t], in0=stat[:, 6 + t:7 + t],
                                                              scalar1=EPS, scalar2=None, op0=ALU.add)),
                      reads=[buf("fss%d" % t)], writes=[buf("fss%d" % t)])
                S.add("pool", (lambda e, t=t: e.tensor_tensor(out=stat[:, 6 + t:7 + t], in0=stat[:, 6 + t:7 + t],
                                                              in1=mhalf[:], op=ALU.pow)),
                      reads=[buf("fss%d" % t), buf("mhalf")], writes=[buf("fss%d" % t)])
                S.add("dve", (lambda e, slot=slot, t=t: e.scalar_tensor_tensor(
                    out=xs[slot][:], in0=xs[slot][:], scalar=stat[:, 6 + t:7 + t], in1=fg[:], op0=ALU.mult, op1=ALU.mult)),
                    reads=[bxs[slot], buf("fss%d" % t), buf("fg")], writes=[bxs[slot]])
                S.add("sp", (lambda e, slot=slot, r0=r0: e.dma_start(out=y_d[r0:r0 + 128, :], in_=xs[slot][:])),
                      reads=[bxs[slot]], dma="xs%d" % slot, is_out=True)

        def HS(b):
            return (b + 1) % 2

        S.tag = "setup"
        S.add("pool", lambda e: e.memset(mhalf[:], -0.5), writes=[buf("mhalf")])
        small(gcol[:], gcol_d, "gcol")
        small(tg[:, 0:128], ident_d, "identf", "tg")
        for cg in (1, 2, 0, 3):
            load_cg(cg)
        S.tag = "h0"
        phase1a(xh_d, 0)
        for cg in (5, 6, 4, 7):
            load_cg(cg)
        S.add("dve", lambda e: e.memset(Vaug[:, :, :, 64:65], 1.0), writes=bV)
        S.add("dve", lambda e: e.memset(qTz[64:128, :, 0, :], 0.0), writes=[bqTz])
        S.add("dve", lambda e: e.memset(qTz[0:64, :, 1, :], 0.0), writes=[bqTz])
        S.add("dve", lambda e: e.tensor_copy(out=ident[:], in_=tg[:, 0:128]), reads=[btz], writes=[buf("ident")])
        phase1b(0)
        S.tag = "b0.P1"
        phase1a(x_d, 0)
        load_w(Watt, w_att_d, bWatt, "watt")
        load_w(Wconv, w_conv_d, bWconv, "wconv")
        for cg in (8, 10, 9, 11):
            load_cg(cg)
        load_w(Wout, w_out_d, bWout, "wout")
        small(hmask[:], hmask_d, "hmask")
        small(cfar[:], cfar_d, "cfar")
        small(cwt[:], cwt_d, "cwt")
        small(cb[:], cb_d, "cb")
        S.tag = "h0"
        for c in chunks_kv(0, -4):
            c()
        S.tag = "b0.P1"
        phase1b(HS(0))
        S.tag = "setup"
        setup_tables()
        S.tag = "b0.P2"
        for c in chunks_kv(HS(0), 0):
            c()
        S.tag = "h1"
        phase1a(xh_d, TB)
        S.tag = "b0.P2"
        phase2_qz(HS(0))
        S.tag = "h1"
        phase1b(0)
        conv_hist_init(0)
        if nblk > 1:
            S.tag = "b1.P1"
            phase1a(x_d, TB)
        S.tag = "setup"
        small(fg[:], fg_d, "fg")
        S.tag = "b0.P3"
        for c in chunks_conv(HS(0), 0):
            c()
        S.tag = "h1"
        for c in chunks_kv(0, -2):
            c()
        if nblk > 1:
            S.tag = "b1.P1"
            phase1b(HS(1))
        for blk in range(nblk):
            hs = HS(blk)
            nb = blk + 1
            S.tag = "b%d.P4" % blk
            fill4, fill5 = [], []
            if nb < nblk:
                ccv = chunks_conv(HS(nb), nb % 2)
                fill4 = chunks_kv(HS(nb), 2 * nb) + ccv[:N_CONV_IN_P4]
                fill5 = ccv[N_CONV_IN_P4:]
            phase4(blk, fill4)
            if nb < nblk:
                S.tag = "b%d.P2q" % nb
                phase2_qz(HS(nb))
            if blk + 2 < nblk:
                S.tag = "b%d.P1" % (blk + 2)
                phase1a(x_d, (blk + 2) * TB)
            S.tag = "b%d.P6" % blk
            phase6_load(blk)
            S.tag = "b%d.P5" % blk
            phase5(hs, blk % 2, fill5)
            if blk + 2 < nblk:
                S.tag = "b%d.P1" % (blk + 2)
                phase1b(hs)
            S.tag = "b%d.P6" % blk
            phase6(blk)

        S.emit(nc, es)
    return nc


def _bias_index_table():
    p = np.arange(128)[:, None]
    jq = np.arange(128)[None, :]
    far = np.full((128, 128), 256, dtype=np.int64)
    far[(p < 64) & (jq >= 64)] = -1
    relA = 128 + jq - p
    nearA = np.minimum(relA, 128) + 128
    relB = jq - p
    nearB = relB + 128
    nearB = np.where((p >= 64) & (jq < 64), -1, nearB)
    return np.concatenate([far, nearA, nearB], axis=1)


def kernel(x, norm_g, w_in, rel_bias, w_att_out, conv_w, conv_b, w_conv_out, w_out, final_norm_g):
    x = np.asarray(x, dtype=np.float32)
    f32 = np.float32
    w_in0 = np.ascontiguousarray(np.asarray(w_in, f32)[0])
    w_att0 = np.ascontiguousarray(np.asarray(w_att_out, f32)[0])
    w_conv0 = np.ascontiguousarray(np.asarray(w_conv_out, f32)[0])
    w_out0 = np.ascontiguousarray(np.asarray(w_out, f32)[0])
    g = np.asarray(norm_g, f32)[0]
    fgv = np.asarray(final_norm_g, f32)
    rb = np.asarray(rel_bias, f32)[0]
    cw = np.asarray(conv_w, f32)[0]
    cbv = np.asarray(conv_b, f32)[0]

    fg_bc = np.ascontiguousarray(np.broadcast_to(fgv[None, :], (128, D)))
    gcol = np.ascontiguousarray(g.reshape(8, 128).T)
    cfar = np.ascontiguousarray(np.broadcast_to(rb[:, 256][None, :], (128, 8)))
    cwt = np.ascontiguousarray(cw.reshape(3, 4, 128).transpose(2, 1, 0).reshape(128, 12))
    cbt = np.ascontiguousarray(cbv.reshape(4, 128).T)
    ident = np.eye(128, dtype=f32)
    idx = _bias_index_table()
    gathered = rb[:, np.maximum(idx, 0)]
    btab = np.where(idx[None, :, :] >= 0, gathered, f32(NEG)).astype(f32)
    btab = np.ascontiguousarray(btab.transpose(1, 0, 2).reshape(128, 8 * 384))

    in_maps = []
    for c in range(N_CORES):
        b, seg = c // 4, c % 4
        t0 = seg * TOK
        xc = np.ascontiguousarray(x[b, t0:t0 + TOK, :])
        if seg == 0:
            xh = np.zeros((HALO, D), f32)
            hm = np.tile(np.array([[NEG, 0.0]], f32), (128, 1))
        else:
            xh = np.ascontiguousarray(x[b, t0 - HALO:t0, :])
            hm = np.tile(np.array([[0.0, 1.0]], f32), (128, 1))
        in_maps.append({
            "x": xc, "xh": xh, "w_in": w_in0, "w_att": w_att0, "w_conv": w_conv0, "w_out": w_out0,
            "fg": fg_bc, "gcol": gcol, "cfar": cfar, "hmask": np.ascontiguousarray(hm), "cwt": cwt, "cb": cbt,
            "ident": ident, "btab": btab,
        })
    nc = build_program()
    res = run_bass_kernel_spmd(nc, in_maps, core_ids=list(range(N_CORES)))
    out = np.empty((NB, SEQ, D), dtype=np.float32)
    for c in range(N_CORES):
        b, seg = c // 4, c % 4
        out[b, seg * TOK:(seg + 1) * TOK, :] = res.results[c]["y"]
    return out
```
